# Optimizing a Trainium2 kernel written in Bass

```python
import jax, jax.numpy as jnp
from jax import lax
import numpy as np

D_MODEL = 1024
BATCH = 8
SEQ = 2048
DEPTH = 2

N_AB_LAYERS = (DEPTH + 1) // 2
N_C_LAYERS = DEPTH // 2

M_HEADS = 4
M_QK_DIM = 64
M_V_DIM = 128
M_CHUNK = 64
GATE_SOFTCAP = 15.0
G_HEADS = 4
G_K_DIM = 128
G_V_DIM = 128
G_CHUNK = 64
SHORT_CONV = 4
C_HEADS = 8
C_K_DIM = D_MODEL // C_HEADS
C_V_DIM = D_MODEL // C_HEADS
C_CHUNK = 32
D_FF = 2816
FFN_CONV = 3
EPS = 1e-6

M_QK = M_HEADS * M_QK_DIM
M_V = M_HEADS * M_V_DIM
G_QK = G_HEADS * G_K_DIM
G_V = G_HEADS * G_V_DIM
AB_SPLITS = (M_QK, M_QK, M_V, M_V, M_HEADS, M_HEADS, G_QK, G_QK, G_V, G_V, G_HEADS, G_HEADS)
AB_SPLIT_IDX = tuple(int(s) for s in np.cumsum(AB_SPLITS)[:-1])
AB_IN = sum(AB_SPLITS)
AB_MIX = M_V + G_V
C_K_TOTAL = C_HEADS * C_K_DIM
C_MIX = C_HEADS * C_V_DIM
C_IN = 2 * C_K_TOTAL + 2 * C_MIX

kernel_name = "hybrid_mlstm_gdn_hgrn2_convffn"


def rms_norm(x, gain):
    xf = x.astype(jnp.float32)
    y = xf * lax.rsqrt(jnp.mean(jnp.square(xf), -1, keepdims=True) + EPS)
    return (y * gain.astype(jnp.float32)).astype(x.dtype)


def head_rms_norm(x, gain):
    B, S, H, d = x.shape
    y = x * lax.rsqrt(jnp.mean(jnp.square(x), -1, keepdims=True) + EPS)
    return y.reshape(B, S, H * d) * gain.astype(jnp.float32)


def l2_norm(x):
    return x * lax.rsqrt(jnp.sum(jnp.square(x), -1, keepdims=True) + EPS)


def softcap(x, cap):
    return cap * jnp.tanh(x / cap)


def causal_dwconv(x, w):
    K, C = w.shape
    return lax.conv_general_dilated(
        x, w[:, None, :].astype(x.dtype), window_strides=(1,), padding=[(K - 1, 0)],
        dimension_numbers=("NWC", "WIO", "NWC"), feature_group_count=C)


def to_heads(t, n_heads):
    B, S, C = t.shape
    return t.reshape(B, S, n_heads, C // n_heads)


def to_chunks(t, L):
    B, S, H = t.shape[:3]
    t = t.reshape((B, S // L, L, H) + t.shape[3:])
    return jnp.moveaxis(t, (1, 3), (0, 2))


def from_chunks(t):
    N, B, H, L = t.shape[:4]
    t = jnp.moveaxis(t, (0, 2), (1, 3))
    return t.reshape((B, N * L, H) + t.shape[4:])


def mlstm_chunkwise(q, k, v, i_pre, f_pre):
    B, S, H, DK = q.shape
    DV = v.shape[-1]
    L = M_CHUNK
    q = q * DK ** -0.5
    qc, kc, vc = to_chunks(q, L), to_chunks(k, L), to_chunks(v, L)
    lfc = to_chunks(jax.nn.log_sigmoid(f_pre), L)
    igc = to_chunks(i_pre, L)
    causal = jnp.tril(jnp.ones((L, L), bool))

    def step(carry, inp):
        C, n, m = carry
        q_, k_, v_, lf_, ig_ = inp
        b = jnp.cumsum(lf_, -1)
        log_d = jnp.where(causal, b[..., :, None] - b[..., None, :] + ig_[..., None, :], -jnp.inf)
        log_inter = b + m[..., None]
        m_t = jnp.maximum(jnp.max(log_d, -1), log_inter)
        d = jnp.exp(log_d - m_t[..., None])
        w_inter = jnp.exp(log_inter - m_t)
        s = jnp.einsum("bhld,bhsd->bhls", q_, k_) * d
        num = jnp.einsum("bhls,bhsv->bhlv", s, v_) + w_inter[..., None] * jnp.einsum("bhld,bhdv->bhlv", q_, C)
        den = jnp.sum(s, -1) + w_inter * jnp.einsum("bhld,bhd->bhl", q_, n)
        h = num / jnp.maximum(jnp.abs(den), jnp.exp(-m_t))[..., None]
        b_last = b[..., -1]
        log_w = b_last[..., None] - b + ig_
        m_new = jnp.maximum(b_last + m, jnp.max(log_w, -1))
        wk = jnp.exp(log_w - m_new[..., None])
        carry_decay = jnp.exp(b_last + m - m_new)
        C = carry_decay[..., None, None] * C + jnp.einsum("bhl,bhld,bhlv->bhdv", wk, k_, v_)
        n = carry_decay[..., None] * n + jnp.einsum("bhl,bhld->bhd", wk, k_)
        return (C, n, m_new), h

    init = (jnp.zeros((B, H, DK, DV), jnp.float32), jnp.zeros((B, H, DK), jnp.float32),
            jnp.zeros((B, H), jnp.float32))
    _, h = lax.scan(step, init, (qc, kc, vc, lfc, igc))
    return from_chunks(h)


def gated_delta_chunkwise(q, k, v, g, beta):
    B, S, H, DK = q.shape
    DV = v.shape[-1]
    L = G_CHUNK
    q = q * DK ** -0.5
    qc, kc, vc = to_chunks(q, L), to_chunks(k, L), to_chunks(v, L)
    bc = to_chunks(beta, L)
    G = jnp.cumsum(to_chunks(g, L), -1)
    incl = jnp.tril(jnp.ones((L, L), bool))
    strict = jnp.tril(jnp.ones((L, L), bool), -1)
    diff = G[..., :, None] - G[..., None, :]
    decay = jnp.where(incl, jnp.exp(jnp.where(incl, diff, 0.0)), 0.0)
    kk = jnp.einsum("nbhld,nbhsd->nbhls", kc, kc)
    a_mat = jnp.where(strict, bc[..., :, None] * kk * decay, 0.0) + jnp.eye(L, dtype=jnp.float32)
    rhs = jnp.concatenate([vc * bc[..., None], kc * (bc * jnp.exp(G))[..., None]], -1)
    sol = lax.linalg.triangular_solve(a_mat, rhs, left_side=True, lower=True, unit_diagonal=True)
    u, w = sol[..., :DV], sol[..., DV:]
    attn = jnp.einsum("nbhld,nbhsd->nbhls", qc, kc) * decay
    q_dec = qc * jnp.exp(G)[..., None]
    k_dec = kc * jnp.exp(G[..., -1:] - G)[..., None]
    g_last = jnp.exp(G[..., -1])

    def step(state, inp):
        u_, w_, attn_, qd_, kd_, gl_ = inp
        v_new = u_ - jnp.einsum("bhld,bhdv->bhlv", w_, state)
        o = jnp.einsum("bhld,bhdv->bhlv", qd_, state) + jnp.einsum("bhls,bhsv->bhlv", attn_, v_new)
        state = gl_[..., None, None] * state + jnp.einsum("bhld,bhlv->bhdv", kd_, v_new)
        return state, o

    _, o = lax.scan(step, jnp.zeros((B, H, DK, DV), jnp.float32), (u, w, attn, q_dec, k_dec, g_last))
    return from_chunks(o)


def hgrn2_chunkwise(q, k, v, log_f):
    B, S, H, DK = q.shape
    DV = v.shape[-1]
    L = C_CHUNK
    q = q * DK ** -0.5
    qc, kc, vc = to_chunks(q, L), to_chunks(k, L), to_chunks(v, L)
    G = jnp.cumsum(to_chunks(log_f, L), -2)
    q_dec = qc * jnp.exp(G)
    k_dec = kc * jnp.exp(G[..., -1:, :] - G)
    g_last = jnp.exp(G[..., -1, :])
    incl = jnp.tril(jnp.ones((L, L), bool))[:, :, None]

    def step(state, inp):
        q_, k_, v_, G_, qd_, kd_, gl_ = inp
        diff = G_[..., :, None, :] - G_[..., None, :, :]
        pair_decay = jnp.exp(jnp.where(incl, diff, -jnp.inf))
        attn = jnp.einsum("bhld,bhlsd,bhsd->bhls", q_, pair_decay, k_)
        o = jnp.einsum("bhld,bhdv->bhlv", qd_, state) + jnp.einsum("bhls,bhsv->bhlv", attn, v_)
        state = gl_[..., :, None] * state + jnp.einsum("bhld,bhlv->bhdv", kd_, v_)
        return state, o

    _, o = lax.scan(step, jnp.zeros((B, H, DK, DV), jnp.float32), (qc, kc, vc, G, q_dec, k_dec, g_last))
    return from_chunks(o)


def hgrn_lower_bounds(lb_logits):
    p = jax.nn.softmax(lb_logits.astype(jnp.float32), axis=0)
    return jnp.cumsum(p, axis=0) - p[0]


def ab_mixer(u, w_in, conv_m, m_gate_bias, m_norm, conv_g, g_a_log, g_dt_bias, g_norm, w_out):
    f32 = jnp.float32
    proj = u @ w_in
    m_q, m_k, m_v, m_o, m_i, m_f, g_q, g_k, g_v, g_gate, g_a, g_b = jnp.split(proj, AB_SPLIT_IDX, axis=-1)
    m_qk = jax.nn.silu(causal_dwconv(jnp.concatenate([m_q, m_k], -1), conv_m))
    m_q, m_k = jnp.split(m_qk, 2, axis=-1)
    gb = m_gate_bias.astype(f32)
    i_pre = softcap(m_i.astype(f32) + gb[0], GATE_SOFTCAP)
    f_pre = softcap(m_f.astype(f32) + gb[1], GATE_SOFTCAP)
    h_m = mlstm_chunkwise(to_heads(m_q, M_HEADS).astype(f32), to_heads(m_k, M_HEADS).astype(f32),
                          to_heads(m_v, M_HEADS).astype(f32), i_pre, f_pre)
    y_m = jax.nn.sigmoid(m_o.astype(f32)) * head_rms_norm(h_m, m_norm)
    g_qkv = jax.nn.silu(causal_dwconv(jnp.concatenate([g_q, g_k, g_v], -1), conv_g))
    g_q, g_k, g_v = jnp.split(g_qkv, 3, axis=-1)
    log_decay = -jnp.exp(g_a_log.astype(f32)) * jax.nn.softplus(g_a.astype(f32) + g_dt_bias.astype(f32))
    beta = jax.nn.sigmoid(g_b.astype(f32))
    h_g = gated_delta_chunkwise(l2_norm(to_heads(g_q, G_HEADS).astype(f32)),
                                l2_norm(to_heads(g_k, G_HEADS).astype(f32)),
                                to_heads(g_v, G_HEADS).astype(f32), log_decay, beta)
    y_g = head_rms_norm(h_g, g_norm) * jax.nn.silu(g_gate.astype(f32))
    y = jnp.concatenate([y_m, y_g], -1).astype(u.dtype)
    return y @ w_out


def hgrn2_mixer(u, w_in, lower_bound, norm_gain, w_out):
    f32 = jnp.float32
    q, f, i, gate = jnp.split(u @ w_in, [C_K_TOTAL, 2 * C_K_TOTAL, 2 * C_K_TOTAL + C_MIX], axis=-1)
    lb = lower_bound.astype(f32)
    ff = f.astype(f32)
    log_f = jnp.logaddexp(jnp.log(lb), jnp.log1p(-lb) + jax.nn.log_sigmoid(ff))
    k = (1.0 - lb) * jax.nn.sigmoid(-ff)
    o = hgrn2_chunkwise(to_heads(jax.nn.silu(q.astype(f32)), C_HEADS), to_heads(k, C_HEADS),
                        to_heads(i.astype(f32), C_HEADS), to_heads(log_f, C_HEADS))
    y = head_rms_norm(o, norm_gain) * jax.nn.silu(gate.astype(f32))
    return y.astype(u.dtype) @ w_out


def conv_ffn(u, w_up, conv_w, conv_b, w_down):
    hdn = causal_dwconv(u @ w_up, conv_w) + conv_b
    gate, val = jnp.split(hdn, 2, axis=-1)
    return (jax.nn.silu(gate) * val) @ w_down


def setup_inputs(seed: int = 0) -> dict:
    key = jax.random.key(seed)
    ks = jax.random.split(key, 24)
    f32 = jnp.float32

    def nrm(k, shape, scale):
        return jax.random.normal(k, shape, f32) * scale

    dt = jnp.exp(jax.random.uniform(ks[9], (N_AB_LAYERS, G_HEADS), f32, np.log(1e-3), np.log(1e-1)))
    return {
        "x": nrm(ks[0], (BATCH, SEQ, D_MODEL), 1.0),
        "norm_gains": 1.0 + nrm(ks[1], (DEPTH, 4, D_MODEL), 0.05),
        "ab_w_in": nrm(ks[2], (N_AB_LAYERS, D_MODEL, AB_IN), D_MODEL ** -0.5),
        "ab_conv_m": nrm(ks[3], (N_AB_LAYERS, SHORT_CONV, 2 * M_QK), SHORT_CONV ** -0.5),
        "ab_m_gate_bias": jnp.stack([nrm(ks[4], (N_AB_LAYERS, M_HEADS), 0.1),
                                     jnp.linspace(3.0, 6.0, M_HEADS, dtype=f32) + nrm(ks[5], (N_AB_LAYERS, M_HEADS), 0.1)], axis=1),
        "ab_m_norm": 1.0 + nrm(ks[6], (N_AB_LAYERS, M_V), 0.05),
        "ab_conv_g": nrm(ks[7], (N_AB_LAYERS, SHORT_CONV, 2 * G_QK + G_V), SHORT_CONV ** -0.5),
        "ab_g_a_log": jnp.log(jax.random.uniform(ks[8], (N_AB_LAYERS, G_HEADS), f32, 1.0, 16.0)),
        "ab_g_dt_bias": dt + jnp.log(-jnp.expm1(-dt)),
        "ab_g_norm": 1.0 + nrm(ks[10], (N_AB_LAYERS, G_V), 0.05),
        "ab_w_out": nrm(ks[11], (N_AB_LAYERS, AB_MIX, D_MODEL), AB_MIX ** -0.5),
        "c_w_in": nrm(ks[12], (N_C_LAYERS, D_MODEL, C_IN), D_MODEL ** -0.5),
        "c_lb_logits": nrm(ks[13], (DEPTH, C_K_TOTAL), 0.5),
        "c_norm": 1.0 + nrm(ks[14], (N_C_LAYERS, C_MIX), 0.05),
        "c_w_out": nrm(ks[15], (N_C_LAYERS, C_MIX, D_MODEL), C_MIX ** -0.5),
        "ffn_w_up": nrm(ks[16], (DEPTH, D_MODEL, 2 * D_FF), D_MODEL ** -0.5),
        "ffn_conv_w": nrm(ks[17], (DEPTH, FFN_CONV, 2 * D_FF), FFN_CONV ** -0.5),
        "ffn_conv_b": nrm(ks[18], (DEPTH, 2 * D_FF), 0.02),
        "ffn_w_down": nrm(ks[19], (DEPTH, D_FF, D_MODEL), D_FF ** -0.5),
    }


def reference(x, norm_gains, ab_w_in, ab_conv_m, ab_m_gate_bias, ab_m_norm, ab_conv_g, ab_g_a_log,
              ab_g_dt_bias, ab_g_norm, ab_w_out, c_w_in, c_lb_logits, c_norm, c_w_out,
              ffn_w_up, ffn_conv_w, ffn_conv_b, ffn_w_down):
    lower_bounds = hgrn_lower_bounds(c_lb_logits)
    for layer in range(DEPTH):
        j = layer // 2
        u = rms_norm(x, norm_gains[layer, 0])
        if layer % 2 == 0:
            mix = ab_mixer(u, ab_w_in[j], ab_conv_m[j], ab_m_gate_bias[j], ab_m_norm[j], ab_conv_g[j],
                           ab_g_a_log[j], ab_g_dt_bias[j], ab_g_norm[j], ab_w_out[j])
        else:
            mix = hgrn2_mixer(u, c_w_in[j], lower_bounds[layer], c_norm[j], c_w_out[j])
        x = x + rms_norm(mix, norm_gains[layer, 1])
        u = rms_norm(x, norm_gains[layer, 2])
        ff = conv_ffn(u, ffn_w_up[layer], ffn_conv_w[layer], ffn_conv_b[layer], ffn_w_down[layer])
        x = x + rms_norm(ff, norm_gains[layer, 3])
    return x
```

```python
import numpy as np
import concourse.bass as bass
import concourse.mybir as mybir
from concourse.bass_utils import run_bass_kernel_spmd

F32 = mybir.dt.float32
BF16 = mybir.dt.bfloat16
AF = mybir.ActivationFunctionType
ALU = mybir.AluOpType
DSZ = {F32: 4, BF16: 2}

N_DMA_SEMS = 24


def _box(ap):
    t = ap.tensor
    kind = type(t).__name__
    if kind.startswith("DRam"):
        return None
    if kind.startswith("PSum"):
        return (t.name, 0, 128, 0, 1 << 30)
    row = 1
    for s in list(t.shape)[1:]:
        row *= int(s)
    dsz = DSZ[ap.dtype] if ap.dtype in DSZ else 4
    tds = DSZ.get(t.dtype, 4)
    off = int(ap.offset)
    p0 = off // row
    f0 = off % row
    aps = [(int(s), int(c)) for (s, c) in ap.ap]
    pc = aps[0][1]
    ext = 1
    for (s, c) in aps[1:]:
        ext += abs(s) * (c - 1)
    return (t.name, p0, p0 + pc, f0 * tds, (f0 + ext) * tds)


class _Op:
    __slots__ = ("eng", "fn", "deps", "waits", "signal", "token", "dma", "id", "presem")

    def __init__(self, eng, fn, dma):
        self.eng = eng
        self.fn = fn
        self.dma = dma
        self.deps = set()
        self.waits = []
        self.signal = False
        self.token = None
        self.presem = None


class Prog:
    ENGS = ("pe", "act", "dve", "pool", "sp")

    def __init__(self):
        self.ops = []
        self.hist = {}
        self.out_dma_ops = []

    def add(self, eng, fn, reads=(), writes=(), dma=False, is_out=False):
        op = _Op(eng, fn, dma)
        op.id = len(self.ops)
        self.ops.append(op)
        rb = [b for b in (_box(a) for a in reads) if b is not None]
        wb = [b for b in (_box(a) for a in writes) if b is not None]
        for (name, p0, p1, f0, f1) in rb:
            for h in self.hist.get(name, ()):
                if h[5] and h[0] < p1 and p0 < h[1] and h[2] < f1 and f0 < h[3]:
                    op.deps.add(h[4])
        for (name, p0, p1, f0, f1) in wb:
            for h in self.hist.get(name, ()):
                if h[0] < p1 and p0 < h[1] and h[2] < f1 and f0 < h[3]:
                    op.deps.add(h[4])
        op.deps.discard(op.id)
        for (name, p0, p1, f0, f1) in wb:
            lst = self.hist.setdefault(name, [])
            lst[:] = [h for h in lst if not (p0 <= h[0] and h[1] <= p1 and f0 <= h[2] and h[3] <= f1)]
            lst.append((p0, p1, f0, f1, op.id, True, eng))
        for (name, p0, p1, f0, f1) in rb:
            lst = self.hist.setdefault(name, [])
            lst[:] = [h for h in lst if not ((not h[5]) and h[6] == eng and (not dma)
                                             and p0 <= h[0] and h[1] <= p1 and f0 <= h[2] and h[3] <= f1
                                             and not self.ops[h[4]].dma)]
            lst.append((p0, p1, f0, f1, op.id, False, eng))
        if is_out:
            self.out_dma_ops.append(op.id)
        return op

    def finalize(self):
        ops = self.ops
        for op in ops:
            for d in op.deps:
                p = ops[d]
                if p.dma:
                    continue
                if p.eng == op.eng and p.eng == "pe" and not op.dma:
                    continue
                p.signal = True
        counts = {e: 0 for e in self.ENGS}
        dma_counts = [0] * N_DMA_SEMS
        dma_rr = 0
        waited = {e: {} for e in self.ENGS}
        for op in ops:
            w = {}
            if op.dma:
                s = dma_rr % N_DMA_SEMS
                dma_rr += 1
                key = ("dma", s)
                if dma_counts[s] > 0:
                    w[key] = dma_counts[s]
                dma_counts[s] += 16
                op.token = (key, dma_counts[s])
            elif op.signal:
                counts[op.eng] += 1
                op.token = (("eng", op.eng), counts[op.eng])
            for d in sorted(op.deps):
                p = ops[d]
                if (not p.dma) and (not op.dma) and p.eng == op.eng and p.eng == "pe":
                    continue
                key, val = p.token
                if w.get(key, 0) < val:
                    w[key] = val
            wd = waited[op.eng]
            op.waits = []
            for key, val in w.items():
                if wd.get(key, 0) < val:
                    wd[key] = val
                    op.waits.append((key, val))
        self.final_waits = {}
        for oid in self.out_dma_ops:
            key, val = ops[oid].token
            if self.final_waits.get(key, 0) < val:
                self.final_waits[key] = val

    def emit(self, nc, block, sems):
        per = {e: [op for op in self.ops if op.eng == e] for e in self.ENGS}
        final_waits = self.final_waits

        def body(engname):
            def _f(eng):
                for op in per[engname]:
                    for key, val in op.waits:
                        eng.wait_ge(sems[key], val)
                    inst = op.fn(eng)
                    if op.dma:
                        inst.then_inc(sems[op.token[0]], 16)
                    elif op.signal:
                        inst.then_inc(sems[op.token[0]], 1)
                if engname == "sp":
                    for key, val in final_waits.items():
                        eng.wait_ge(sems[key], val)
            return _f

        block.tensor(body("pe"))
        block.scalar(body("act"))
        block.vector(body("dve"))
        block.gpsimd(body("pool"))
        block.sync(body("sp"))


class KB:
    def __init__(self, nc):
        self.nc = nc
        self.P = Prog()

    @staticmethod
    def _aps(*xs):
        return [x for x in xs if x is not None and not isinstance(x, (int, float))]

    def mm(self, out, lhsT, rhs, start=True, stop=True):
        self.P.add("pe", lambda e: e.matmul(out, lhsT, rhs, start=start, stop=stop),
                   reads=[lhsT, rhs], writes=[out])

    def transpose(self, out, in_, ident):
        self.P.add("pe", lambda e: e.transpose(out, in_, ident), reads=[in_, ident], writes=[out])

    def act(self, out, in_, func, bias=None, scale=None, eng="act"):
        kw = {}
        if bias is not None:
            kw["bias"] = bias
        if scale is not None:
            kw["scale"] = scale
        self.P.add(eng, lambda e: e.activation(out, in_, func, **kw),
                   reads=self._aps(in_, bias, scale), writes=[out])

    def tt(self, out, in0, in1, op, eng="dve"):
        self.P.add(eng, lambda e: e.tensor_tensor(out, in0, in1, op), reads=[in0, in1], writes=[out])

    def ts(self, out, in0, s1, s2, op0, op1=None, eng="dve"):
        if op1 is None:
            self.P.add(eng, lambda e: e.tensor_scalar(out, in0, s1, None, op0),
                       reads=self._aps(in0, s1), writes=[out])
        else:
            self.P.add(eng, lambda e: e.tensor_scalar(out, in0, s1, s2, op0, op1),
                       reads=self._aps(in0, s1, s2), writes=[out])

    def stt(self, out, in0, scalar, in1, op0, op1):
        self.P.add("dve", lambda e: e.scalar_tensor_tensor(out, in0, scalar, in1, op0, op1),
                   reads=self._aps(in0, scalar, in1), writes=[out])

    def scan(self, out, d0, d1, initial, op0, op1):
        self.P.add("dve", lambda e: e.tensor_tensor_scan(out, d0, d1, initial, op0, op1),
                   reads=self._aps(d0, d1, initial), writes=[out])

    def copy(self, out, in_, eng="dve"):
        if eng == "act":
            self.P.add(eng, lambda e: e.copy(out, in_), reads=[in_], writes=[out])
        else:
            self.P.add(eng, lambda e: e.tensor_copy(out, in_), reads=[in_], writes=[out])

    def recip(self, out, in_):
        self.P.add("dve", lambda e: e.reciprocal(out, in_), reads=[in_], writes=[out])

    def memset(self, ap, val, eng="pool"):
        self.P.add(eng, lambda e: e.memset(ap, val), reads=[], writes=[ap])

    def dma(self, out, in_, eng="sp", is_out=False):
        self.P.add(eng, lambda e: e.dma_start(out=out, in_=in_), reads=[in_], writes=[out], dma=True,
                   is_out=is_out)


D = 1024
S = 2048
TB = 1024
NBLK = S // TB
NCH = TB // 64
EPS = 1e-6
DFF = 2816
NPAIR = DFF // 128

R_NG, R_FCW, R_FCB, R_CM, R_CG, R_MN, R_GN, R_LB, R_CN = 0, 64, 328, 416, 432, 480, 484, 488, 504

CB_ID, CB_ONE, CB_MI, CB_MSU, CB_MSL, CB_IB, CB_SCAN, CB_NEG, CB_POS, CB_M2, CB_N = 0, 128, 256, 768, 1280, 1792, 2304, 3328, 3392, 3456, 3968
CF_ID, CF_SEL, CF_N = 0, 128, 128 + 12 * 128


def unit_table():
    tab = {}
    idx = 0

    def add(name, nk, nc_):
        nonlocal idx
        tab[name] = (idx, nk, nc_)
        idx += 1
    for n in ("mqk0", "mqk1", "mv0", "mv1", "mo0", "mo1", "gq0", "gq1", "gk0", "gk1", "gv0", "gv1",
              "gg0", "gg1"):
        add(n, 8, 256)
    add("gates", 8, 16)
    for i in range(4):
        add("abwo%d" % i, 8, 256)
    for h in range(8):
        add("cqf%d" % h, 8, 256)
        add("cig%d" % h, 8, 256)
    for i in range(4):
        add("cwo%d" % i, 8, 256)
    for l in range(2):
        for j in range(NPAIR):
            add("up%d_%d" % (l, j), 8, 256)
        for oc in range(8):
            for kh in range(2):
                add("dn%d_%d_%d" % (l, oc, kh), 11, 128)
    return tab, idx


def _pack(W, ks, cols):
    Wr = W.reshape(W.shape[0] // 128, 128, W.shape[1])
    return np.ascontiguousarray(np.transpose(Wr[ks][:, :, cols], (1, 0, 2)))


def pack_weights(inp):
    tab, n = unit_table()
    WP = np.zeros((n, 128, 2048), np.float32)

    def put(name, arr):
        i, nk, nc_ = tab[name]
        assert arr.shape == (128, nk, nc_), (name, arr.shape)
        WP[i, :, : nk * nc_] = arr.reshape(128, nk * nc_)
    k8 = list(range(8))
    w = inp["ab_w_in"][0]
    base = {"mqk": 0, "mv": 512, "mo": 1024, "gq": 1544, "gk": 2056, "gv": 2568, "gg": 3080}
    for nm, b in base.items():
        for i in range(2):
            put("%s%d" % (nm, i), _pack(w, k8, np.arange(b + i * 256, b + (i + 1) * 256)))
    put("gates", _pack(w, k8, np.concatenate([np.arange(1536, 1544), np.arange(3592, 3600)])))
    for i in range(4):
        put("abwo%d" % i, _pack(inp["ab_w_out"][0], k8, np.arange(i * 256, (i + 1) * 256)))
        put("cwo%d" % i, _pack(inp["c_w_out"][0], k8, np.arange(i * 256, (i + 1) * 256)))
    cw = inp["c_w_in"][0]
    for h in range(8):
        put("cqf%d" % h, _pack(cw, k8, np.concatenate([np.arange(h * 128, (h + 1) * 128),
                                                        np.arange(1024 + h * 128, 1024 + (h + 1) * 128)])))
        put("cig%d" % h, _pack(cw, k8, np.concatenate([np.arange(2048 + h * 128, 2048 + (h + 1) * 128),
                                                        np.arange(3072 + h * 128, 3072 + (h + 1) * 128)])))
    for l in range(2):
        wu = inp["ffn_w_up"][l]
        wd = inp["ffn_w_down"][l]
        for j in range(NPAIR):
            put("up%d_%d" % (l, j), _pack(wu, k8, np.concatenate([np.arange(j * 128, (j + 1) * 128),
                                                                   np.arange(DFF + j * 128, DFF + (j + 1) * 128)])))
        for oc in range(8):
            for kh in range(2):
                put("dn%d_%d_%d" % (l, oc, kh), _pack(wd, list(range(kh * 11, kh * 11 + 11)),
                                                      np.arange(oc * 128, (oc + 1) * 128)))
    return WP


def pack_params(inp):
    pv = np.zeros((512, 128), np.float32)
    pv[R_NG:R_NG + 64] = inp["norm_gains"].reshape(64, 128)
    pv[R_FCW:R_FCW + 264] = inp["ffn_conv_w"].reshape(2 * 3 * 44, 128)
    pv[R_FCB:R_FCB + 88] = inp["ffn_conv_b"].reshape(88, 128)
    pv[R_CM:R_CM + 16] = inp["ab_conv_m"][0].reshape(16, 128)
    pv[R_CG:R_CG + 48] = inp["ab_conv_g"][0].reshape(48, 128)
    pv[R_MN:R_MN + 4] = inp["ab_m_norm"][0].reshape(4, 128)
    pv[R_GN:R_GN + 4] = inp["ab_g_norm"][0].reshape(4, 128)
    pv[R_LB:R_LB + 16] = inp["c_lb_logits"].reshape(16, 128)
    pv[R_CN:R_CN + 8] = inp["c_norm"][0].reshape(8, 128)
    gsc = np.zeros((16, 2), np.float32)
    gsc[0:4, 0] = inp["ab_m_gate_bias"][0, 0]
    gsc[4:8, 0] = inp["ab_m_gate_bias"][0, 1]
    gsc[8:12, 0] = inp["ab_g_dt_bias"][0]
    gsc[8:12, 1] = inp["ab_g_a_log"][0]
    return pv, gsc


def make_consts():
    import ml_dtypes
    p = np.arange(128)[:, None]
    cb = np.zeros((128, CB_N), np.float32)
    cb[:, CB_ID:CB_ID + 128] = np.eye(128)
    cb[:, CB_ONE:CB_ONE + 128] = 1.0
    col = np.arange(512)[None, :]
    l = col % 64
    cb[:, CB_MI:CB_MI + 512] = ((p % 64) <= l)
    cb[:, CB_MSU:CB_MSU + 512] = ((p % 64) < l)
    cb[:, CB_MSL:CB_MSL + 512] = (l < (p % 64))
    cb[:, CB_IB:CB_IB + 512] = ((p % 64) == l)
    t = np.arange(1024)[None, :]
    cb[:, CB_SCAN:CB_SCAN + 1024] = np.broadcast_to((t % 64) != 0, (128, 1024))
    cb[:, CB_NEG:CB_NEG + 64] = np.where((p % 64) > np.arange(64)[None, :], -1e30, 0.0)
    cb[:, CB_POS:CB_POS + 64] = np.where(np.arange(64)[None, :] >= (p % 64), 1e30, 0.0)
    j = np.arange(512)[None, :]
    cb[:, CB_M2:CB_M2 + 512] = ((p // 64) == ((j % 128) // 64)) & ((p % 64) <= (j % 64))
    cf = np.zeros((128, CF_N), np.float32)
    cf[:, CF_ID:CF_ID + 128] = np.eye(128)
    sel = np.zeros((12, 16, 128), np.float32)
    for pr in range(2):
        sel[pr, 2 * pr, 0:64] = 1.0
        sel[pr, 2 * pr + 1, 64:128] = 1.0
        sel[2 + pr, 4 + 2 * pr, 0:64] = 1.0
        sel[2 + pr, 4 + 2 * pr + 1, 64:128] = 1.0
    for h in range(4):
        sel[4 + h, 8 + h, :] = 1.0
        sel[8 + h, 12 + h, :] = 1.0
    for i in range(12):
        cf[0:16, CF_SEL + i * 128: CF_SEL + (i + 1) * 128] = sel[i]
    return cb.astype(ml_dtypes.bfloat16), cf


SB_BYTES = 212000


class Arena:
    def __init__(self, SB):
        self.SB = SB
        self.top = 0
        self.marks = []

    def alloc(self, nbytes):
        nbytes = (nbytes + 63) // 64 * 64
        off = self.top
        self.top += nbytes
        assert self.top <= SB_BYTES, ("SBUF arena overflow", self.top)
        return off

    def f32(self, n):
        off = self.alloc(4 * n)
        return self.SB[:, off // 2: off // 2 + 2 * n].bitcast(F32)

    def bf(self, n):
        off = self.alloc(2 * n)
        return self.SB[:, off // 2: off // 2 + n]

    def mark(self):
        self.marks.append(self.top)

    def release(self):
        self.top = self.marks.pop()


def build_program(n_layers=2, dbg=False, stop_after=None):
    import contextlib
    nc = bass.Bass("TRN2", target_bir_lowering=False)
    tab, nunits = unit_table()
    xT_d = nc.dram_tensor("xT", [D, S], F32, kind="ExternalInput").ap()
    wp_d = nc.dram_tensor("wp", [nunits, 128, 2048], F32, kind="ExternalInput").ap()
    pv_d = nc.dram_tensor("pv", [512, 128], F32, kind="ExternalInput").ap()
    gsc_d = nc.dram_tensor("gsc", [16, 2], F32, kind="ExternalInput").ap()
    cb_d = nc.dram_tensor("cb", [128, CB_N], BF16, kind="ExternalInput").ap()
    cf_d = nc.dram_tensor("cf", [128, CF_N], F32, kind="ExternalInput").ap()
    oT_d = nc.dram_tensor("oT", [D, S], F32, kind="ExternalOutput").ap()
    dbg_d = None
    if dbg:
        dbg_d = nc.dram_tensor("dbg", [4, D, S], F32, kind="ExternalOutput").ap()

    with contextlib.ExitStack() as st:
        SB = st.enter_context(nc.sbuf_tensor("SB", [128, SB_BYTES // 2], BF16))
        PS = [st.enter_context(nc.psum_tensor("ps%d" % i, [128, 512], F32)) for i in range(7)]
        PT = st.enter_context(nc.psum_tensor("pst", [128, 1024], BF16))
        sems = {}
        for e in Prog.ENGS:
            sems[("eng", e)] = st.enter_context(nc.semaphore("s_" + e))
        for i in range(N_DMA_SEMS):
            sems[("dma", i)] = st.enter_context(nc.semaphore("d%d" % i))
        block = st.enter_context(nc.Block())
        kb = KB(nc)
        A = Arena(SB)
        psi = [0]

        def ps():
            psi[0] = (psi[0] + 1) % 7
            return PS[psi[0]]

        xT = A.f32(8 * TB).rearrange("p (c t) -> p c t", c=8)
        PC = A.f32(512)
        cb = A.bf(CB_N)
        cf = A.f32(CF_N)
        misc = A.f32(64)
        gsc = A.f32(2)
        st_h = A.f32(8 * 128).rearrange("p (h v) -> p h v", h=8)
        st_m = A.f32(2 * 256).rearrange("p (h v) -> p h v", h=2)
        st_g = A.f32(4 * 128).rearrange("p (h v) -> p h v", h=4)
        halo_f = A.f32(2 * 44 * 2).rearrange("p (l c j) -> p l c j", l=2, c=44)
        halo_m = A.f32(4 * 3).rearrange("p (c j) -> p c j", c=4)
        halo_g = A.f32(12 * 3).rearrange("p (c j) -> p c j", c=12)
        wstage = [A.f32(2048) for _ in range(2)]
        wbf = [A.bf(2048) for _ in range(4)]
        uT = A.bf(8 * TB)
        uT3 = uT.rearrange("p (c t) -> p c t", c=8)
        sqb = A.bf(8 * 512).rearrange("p (c t) -> p c t", c=8)
        rstd = A.f32(512)
        lnv = A.f32(512)

        ident_f = cf[:, CF_ID:CF_ID + 128]
        ident_b = cb[:, CB_ID:CB_ID + 128]
        ones_b = cb[:, CB_ONE:CB_ONE + 128]
        m_incl = cb[:, CB_MI:CB_MI + 512]
        m_su = cb[:, CB_MSU:CB_MSU + 512]
        m_sl = cb[:, CB_MSL:CB_MSL + 512]
        iblk = cb[:, CB_IB:CB_IB + 512]
        scanm = cb[:, CB_SCAN:CB_SCAN + 1024]
        negm = cb[:, CB_NEG:CB_NEG + 64]
        posm = cb[:, CB_POS:CB_POS + 64]
        m2 = cb[:, CB_M2:CB_M2 + 512]
        eps_c = misc[:, 0:1]
        one_c = misc[:, 1:2]
        lbc = misc[:, 8:16]
        omlc = misc[:, 16:24]
        nac = misc[:, 24:25]

        def sel(i):
            return cf[0:16, CF_SEL + i * 128: CF_SEL + (i + 1) * 128]

        import os
        LVL = int(os.environ.get("SETUP_LVL", "99"))
        kb.dma(cb, cb_d)
        kb.dma(cf, cf_d)
        if LVL >= 1:
            kb.dma(gsc[0:16, :], gsc_d)
        kb.memset(misc, 0.0, eng="dve")
        kb.memset(eps_c, EPS, eng="dve")
        kb.memset(one_c, 1.0, eng="dve")
        for t_ in (st_h, st_m, st_g, halo_f, halo_m, halo_g):
            kb.memset(t_, 0.0, eng="dve")
        pstg = wstage[0]
        for i in range(4):
            kb.dma(pstg[:, i * 128:(i + 1) * 128], pv_d[i * 128:(i + 1) * 128, :])
        for i in range(4 if LVL >= 2 else 0):
            p_ = ps()
            kb.transpose(p_[:, 0:128], pstg[:, i * 128:(i + 1) * 128], ident_f)
            kb.copy(PC[:, i * 128:(i + 1) * 128], p_[:, 0:128], eng="act")
        if LVL >= 3:
            kb.tt(lbc, PC[:, R_LB + 8:R_LB + 16], PC[:, R_LB:R_LB + 8], ALU.subtract)
            kb.act(lbc, lbc, AF.Sigmoid)
            kb.ts(omlc, lbc, -1.0, 1.0, ALU.mult, ALU.add)
        if LVL >= 4:
            kb.act(nac[0:16, :], gsc[0:16, 1:2], AF.Exp)
            kb.ts(nac[0:16, :], nac[0:16, :], -1.0, None, ALU.mult)

        wcnt = [0]

        def wload(name):
            i, nk, ncl = tab[name]
            n = nk * ncl
            k = wcnt[0]
            wcnt[0] += 1
            stg = wstage[k % 2]
            dst = wbf[k % 4]
            kb.dma(stg[:, 0:n], wp_d[i, :, 0:n])
            kb.copy(dst[:, 0:n], stg[:, 0:n], eng="pool")
            return dst[:, 0:n].rearrange("p (k c) -> p k c", k=nk)

        def gcol(layer, idx, c):
            r = R_NG + (layer * 4 + idx) * 8 + c
            return PC[:, r:r + 1]

        def rstd_from_sum(p_sum, inv_n, ncols=512):
            kb.act(lnv[:, 0:ncols], p_sum[:, 0:ncols], AF.Ln, bias=eps_c, scale=inv_n)
            kb.act(rstd[:, 0:ncols], lnv[:, 0:ncols], AF.Exp, scale=-0.5)

        def prenorm(layer, idx):
            for t in range(TB // 512):
                tl = slice(t * 512, (t + 1) * 512)
                for c in range(8):
                    kb.act(sqb[:, c, :], xT[:, c, tl], AF.Square)
                p_ = ps()
                for c in range(8):
                    kb.mm(p_[:, :], ones_b, sqb[:, c, :], start=(c == 0), stop=(c == 7))
                rstd_from_sum(p_, 1.0 / D)
                for c in range(8):
                    kb.stt(uT3[:, c, tl], xT[:, c, tl], gcol(layer, idx, c), rstd[:, :], ALU.mult, ALU.mult)

        def out_proj_residual(layer, idx, src3, nk, lhs_fn, mix3):
            for oc in range(8):
                lhs = lhs_fn(oc)
                for t in range(TB // 512):
                    tl = slice(t * 512, (t + 1) * 512)
                    p_ = ps()
                    for k in range(nk):
                        kb.mm(p_[:, :], lhs[k], src3[:, k, tl], start=(k == 0), stop=(k == nk - 1))
                    kb.act(mix3[:, oc, tl], p_[:, :], AF.Copy)
                    kb.act(uT3[:, oc, tl], p_[:, :], AF.Square)
            for t in range(TB // 512):
                tl = slice(t * 512, (t + 1) * 512)
                p2 = ps()
                for c in range(8):
                    kb.mm(p2[:, :], ones_b, uT3[:, c, tl], start=(c == 0), stop=(c == 7))
                rstd_from_sum(p2, 1.0 / D)
                for c in range(8):
                    kb.stt(mix3[:, c, tl], mix3[:, c, tl], gcol(layer, idx, c), rstd[:, :], ALU.mult, ALU.mult)
                    kb.tt(xT[:, c, tl], xT[:, c, tl], mix3[:, c, tl], ALU.add, eng="pool")

        def conv_taps(acc, hraw, K, wcols, bias_col):
            if bias_col is not None:
                kb.ts(acc, hraw[:, K - 1:K - 1 + TB], wcols[K - 1], bias_col, ALU.mult, ALU.add)
            else:
                kb.ts(acc, hraw[:, K - 1:K - 1 + TB], wcols[K - 1], None, ALU.mult)
            for j in range(K - 1):
                kb.stt(acc, hraw[:, j:j + TB], wcols[j], acc, ALU.mult, ALU.add)

        def proj_fm(dst_fn, wtile, col0, evac):
            for t in range(TB // 512):
                tl = slice(t * 512, (t + 1) * 512)
                p_ = ps()
                for k in range(8):
                    kb.mm(p_[:, :], wtile[:, k, col0:col0 + 128], uT3[:, k, tl], start=(k == 0), stop=(k == 7))
                evac(p_, tl)

        def conv_pe(wtile, col0, halo, wcols, K, hb, dg, evac2):
            kb.copy(hb[:, 0:K - 1], halo, eng="act")
            proj_fm(None, wtile, col0,
                    lambda p_, tl: kb.act(hb[:, K - 1 + tl.start:K - 1 + tl.stop], p_[:, :], AF.Copy))
            for j in range(K):
                kb.ts(dg[:, j, :], ident_b, wcols[j], None, ALU.mult)
            kb.copy(halo, hb[:, TB:TB + K - 1], eng="act")
            for t in range(TB // 512):
                tl = slice(t * 512, (t + 1) * 512)
                p2 = ps()
                for j in range(K):
                    kb.mm(p2[:, :], dg[:, j, :], hb[:, j + t * 512:j + t * 512 + 512], start=(j == 0), stop=(j == K - 1))
                evac2(p2, tl)

        def ffn(layer):
            A.mark()
            aT = A.bf(NPAIR * TB).rearrange("p (c t) -> p c t", c=NPAIR)
            A.release()
            A.mark()
            A.bf(NPAIR * TB)
            hb = [[A.bf(2 + TB) for _ in range(2)] for _ in range(2)]
            dg = [[A.bf(3 * 128).rearrange("p (j c) -> p j c", j=3) for _ in range(2)] for _ in range(2)]
            sg = [A.f32(TB) for _ in range(2)]
            prenorm(layer, 2)
            for j in range(NPAIR):
                w = wload("up%d_%d" % (layer, j))
                for half in range(2):
                    ch = half * NPAIR + j
                    wc = [PC[:, R_FCW + (layer * 3 + jj) * 44 + ch: R_FCW + (layer * 3 + jj) * 44 + ch + 1]
                          for jj in range(3)]
                    bc = PC[:, R_FCB + layer * 44 + ch: R_FCB + layer * 44 + ch + 1]
                    sgj = sg[j % 2]
                    if half == 0:
                        ev = lambda p2, tl, sgj=sgj, bc=bc: kb.act(sgj[:, tl], p2[:, :], AF.Silu, bias=bc, scale=1.0)
                    else:
                        ev = lambda p2, tl, sgj=sgj, bc=bc, j=j: kb.stt(aT[:, j, tl], p2[:, :], bc, sgj[:, tl],
                                                                      ALU.add, ALU.mult)
                    conv_pe(w, half * 128, halo_f[:, layer, ch, :], wc, 3, hb[j % 2][half], dg[j % 2][half], ev)
            A.release()
            A.mark()
            aT_ = A.bf(NPAIR * TB)
            mix3 = A.f32(8 * TB).rearrange("p (c t) -> p c t", c=8)

            def dn_lhs(oc):
                w0 = wload("dn%d_%d_0" % (layer, oc))
                w1 = wload("dn%d_%d_1" % (layer, oc))
                return [w0[:, k, :] for k in range(11)] + [w1[:, k, :] for k in range(11)]
            out_proj_residual(layer, 3, aT, NPAIR, dn_lhs, mix3)
            A.release()

        def transpose_to_tok(dst_tok, src_fm):
            for tt in range(TB // 128):
                kb.transpose(PT[:, tt * 128:(tt + 1) * 128], src_fm[:, tt * 128:(tt + 1) * 128], ident_b)
            kb.copy(dst_tok.rearrange("p a b -> p (a b)"), PT[:, 0:TB], eng="act")

        def head_norm_out(p_o, src_is_sbuf, gain_col, gate_bf, y_dst):
            kb.act(sqb[:, 0, :], p_o, AF.Square)
            p2 = ps()
            kb.mm(p2[:, :], ones_b, sqb[:, 0, :])
            rstd_from_sum(p2, 1.0 / 128)
            kb.stt(lnv[:, :], p_o, gain_col, rstd[:, :], ALU.mult, ALU.mult)
            kb.tt(y_dst, lnv[:, :], gate_bf, ALU.mult)

        def chunk_rows(c):
            return slice((c % 2) * 64, (c % 2) * 64 + 64)

        def attn_T(at_sb, lhs_fm, rhs_fm, prow, mask, t):
            p_ = ps()
            for c8 in range(8):
                c = t * 8 + c8
                cs = slice(c * 64, (c + 1) * 64)
                kb.mm(p_[chunk_rows(c), (c8 // 2) * 64:(c8 // 2) * 64 + 64], lhs_fm[prow, cs], rhs_fm[prow, cs])
            kb.tt(at_sb, p_[:, 0:256], mask[:, 0:256], ALU.mult)

        def attn_bd(at_sb, lhs_fm, rhs_fm, prow, t):
            p_ = ps()
            for t4 in range(4):
                ts_ = slice((t * 4 + t4) * 128, (t * 4 + t4 + 1) * 128)
                kb.mm(p_[:, t4 * 128:(t4 + 1) * 128], lhs_fm[prow, ts_], rhs_fm[prow, ts_])
            kb.tt(at_sb[:, 0:512], p_[:, :], m2, ALU.mult)

        def hgrn2(layer, blk):
            A.mark()
            yT = A.bf(8 * TB).rearrange("p (c t) -> p c t", c=8)
            A.mark()
            qs = A.f32(TB)
            fb = A.f32(TB)
            G = A.f32(TB)
            eG = A.f32(TB)
            eGi = A.f32(TB)
            gs = A.bf(TB)
            qd = A.bf(TB)
            ki = A.bf(TB)
            kd = A.bf(TB)
            v_tok = A.bf(TB).rearrange("p (a b) -> p a b", b=128)
            kd_tok = A.bf(TB).rearrange("p (a b) -> p a b", b=128)
            s32 = A.f32(2 * 128).rearrange("p (a b) -> p a b", b=128)
            s_bf = A.bf(NCH * 128).rearrange("p (a b) -> p a b", b=128)
            at_sb = A.bf(512)
            prenorm(layer, 0)
            sc = 128 ** -0.5
            allp = slice(0, 128)
            HL = int(os.environ.get("HL", "99"))
            for h in range(8 if HL >= 99 else 1):
                wqf = wload("cqf%d" % h)
                wig = wload("cig%d" % h)
                proj_fm(None, wqf, 0, lambda p_, tl: kb.act(qs[:, tl], p_[:, :], AF.Silu))
                proj_fm(None, wqf, 128, lambda p_, tl: kb.act(fb[:, tl], p_[:, :], AF.Sigmoid))
                proj_fm(None, wig, 128, lambda p_, tl: kb.act(gs[:, tl], p_[:, :], AF.Silu))
                for g4 in range(2):
                    p_ = ps()
                    for t4 in range(4):
                        tt = g4 * 4 + t4
                        for k in range(8):
                            kb.mm(p_[:, t4 * 128:(t4 + 1) * 128], uT3[:, k, tt * 128:(tt + 1) * 128],
                                  wig[:, k, 0:128], start=(k == 0), stop=(k == 7))
                    kb.copy(v_tok[:, g4 * 4:(g4 + 1) * 4, :].rearrange("p a b -> p (a b)"), p_[:, :], eng="act")
                if HL < 1:
                    continue
                kb.ts(fb, fb, omlc[:, h:h + 1], lbc[:, h:h + 1], ALU.mult, ALU.add)
                kb.act(eG, fb, AF.Ln)
                kb.scan(G, scanm, eG, 0.0, ALU.mult, ALU.add)
                kb.act(eG, G, AF.Exp)
                kb.act(eGi, G, AF.Exp, scale=-1.0)
                kb.stt(qd, qs, -sc, eG, ALU.mult, ALU.mult)
                kb.stt(ki, fb, -1.0, eGi, ALU.add, ALU.mult)
                kb.tt(kd.rearrange("p (c l) -> p c l", l=64), ki.rearrange("p (c l) -> p c l", l=64),
                      eG.rearrange("p (c l) -> p c l", l=64)[:, :, 63:64].to_broadcast([128, NCH, 64]), ALU.mult)
                if HL < 2:
                    continue
                transpose_to_tok(kd_tok, kd)
                if HL < 3:
                    continue
                kb.copy(s32[:, 0, :], st_h[:, h, :], eng="dve")
                for c in range(NCH):
                    tt = c // 2
                    cr = chunk_rows(c)
                    kb.copy(s_bf[:, c, :], s32[:, c % 2, :], eng="act")
                    pkv = ps()
                    o_ = slice(0, 128)
                    kb.mm(pkv[:, o_], kd_tok[cr, tt, :], v_tok[cr, tt, :])
                    kb.stt(s32[:, (c + 1) % 2, :], s32[:, c % 2, :], eG[:, c * 64 + 63:c * 64 + 64],
                           pkv[:, o_], ALU.mult, ALU.add)
                kb.copy(st_h[:, h, :], s32[:, NCH % 2, :], eng="dve")
                if HL < 4:
                    continue
                for t in range(TB // 512):
                    tl = slice(t * 512, (t + 1) * 512)
                    attn_bd(at_sb, ki, qd, allp, t)
                    po = ps()
                    for t4 in range(4):
                        tt = t * 4 + t4
                        oc_ = slice(t4 * 128, (t4 + 1) * 128)
                        kb.mm(po[:, oc_], v_tok[:, tt, :], at_sb[:, oc_], start=True, stop=False)
                        for cc in range(2):
                            c = tt * 2 + cc
                            kb.mm(po[:, t4 * 128 + cc * 64:t4 * 128 + cc * 64 + 64], s_bf[:, c, :],
                                  qd[:, c * 64:(c + 1) * 64], start=False, stop=(cc == 1))
                    head_norm_out(po[:, :], False, PC[:, R_CN + h:R_CN + h + 1], gs[:, tl], yT[:, h, tl])
            A.release()
            mix3 = A.f32(8 * TB).rearrange("p (c t) -> p c t", c=8)

            def lhs(oc):
                if oc % 2 == 0:
                    lhs.w = wload("cwo%d" % (oc // 2))
                return [lhs.w[:, k, (oc % 2) * 128:(oc % 2) * 128 + 128] for k in range(8)]
            out_proj_residual(layer, 1, yT, 8, lhs, mix3)
            A.release()

        def conv_silu_chunk(wtile, col0, halo, wcols, dst, hb, dg):
            conv_pe(wtile, col0, halo, wcols, 4, hb, dg, lambda p2, tl: kb.act(dst[:, tl], p2[:, :], AF.Silu))

        def ab_mixer(layer, blk):
            A.mark()
            yT = A.bf(8 * TB).rearrange("p (c t) -> p c t", c=8)
            A.mark()
            prenorm(layer, 0)
            Gg = A.f32(TB)
            SG = A.f32(TB)
            Gtok = A.f32(8 * 16).rearrange("p (a b) -> p a b", b=16)
            A.mark()
            T15 = A.f32(TB)
            bm = A.f32(TB)
            A.mark()
            pre = A.f32(TB)
            tmp = A.f32(TB)
            r16 = slice(0, 16)
            wg = wload("gates")
            for t in range(TB // 512):
                tl = slice(t * 512, (t + 1) * 512)
                p_ = ps()
                for k in range(8):
                    kb.mm(p_[r16, :], wg[:, k, :], uT3[:, k, tl], start=(k == 0), stop=(k == 7))
                kb.act(pre[r16, tl], p_[r16, :], AF.Identity, bias=gsc[r16, 0:1], scale=1.0)
            kb.act(T15[r16, :], pre[r16, :], AF.Tanh, scale=1.0 / 15.0)
            kb.ts(T15[r16, :], T15[r16, :], 15.0, None, ALU.mult)
            kb.act(tmp[r16, :], T15[r16, :], AF.Exp, scale=-1.0)
            kb.act(tmp[r16, :], tmp[r16, :], AF.Ln, bias=one_c[r16, :], scale=1.0)
            kb.scan(bm[r16, :], scanm[r16, :], tmp[r16, :], 0.0, ALU.mult, ALU.add)
            kb.act(SG[r16, :], pre[r16, :], AF.Sigmoid)
            kb.act(tmp[r16, :], pre[r16, :], AF.Exp)
            kb.act(tmp[r16, :], tmp[r16, :], AF.Ln, bias=one_c[r16, :], scale=1.0)
            kb.ts(tmp[r16, :], tmp[r16, :], nac[r16, :], None, ALU.mult)
            kb.scan(Gg[r16, :], scanm[r16, :], tmp[r16, :], 0.0, ALU.mult, ALU.add)
            for tt in range(8):
                p_ = ps()
                kb.transpose(p_[:, 0:16], Gg[r16, tt * 128:(tt + 1) * 128], ident_f[r16, 0:16])
                kb.copy(Gtok[:, tt, :], p_[:, 0:16], eng="act")
            A.release()

            hb = A.bf(4 + TB)
            dg = A.bf(4 * 128).rearrange("p (j c) -> p j c", j=4)
            acc = A.f32(TB)
            ks = A.f32(TB)
            Abc = A.f32(TB)
            Cbc = A.f32(TB)
            qd = A.bf(TB)
            kt = A.bf(TB)
            kd = A.bf(TB)
            kd_tok = A.bf(TB).rearrange("p (a b) -> p a b", b=128)
            v_tok = A.bf(8 * 256).rearrange("p (a b) -> p a b", b=256)
            og = A.bf(2 * TB).rearrange("p (h t) -> p h t", h=2)
            c32 = A.f32(2 * 256).rearrange("p (a b) -> p a b", b=256)
            c_bf = A.bf(NCH * 256).rearrange("p (a b) -> p a b", b=256)
            at_sb = A.bf(512)
            hm = A.f32(512)
            ad = A.f32(512)
            sc = 64 ** -0.5
            for pr in range(2):
                wv = wload("mv%d" % pr)
                for tt in range(8):
                    if tt % 2 == 0:
                        p_ = ps()
                    for k in range(8):
                        kb.mm(p_[:, (tt % 2) * 256:(tt % 2) * 256 + 256], uT3[:, k, tt * 128:(tt + 1) * 128],
                              wv[:, k, :], start=(k == 0), stop=(k == 7))
                    if tt % 2 == 1:
                        kb.copy(v_tok[:, tt - 1:tt + 1, :], p_[:, :].rearrange("p (a b) -> p a b", b=256), eng="act")
                wo_ = wload("mo%d" % pr)
                for hh in range(2):
                    proj_fm(None, wo_, hh * 128, lambda p_, tl, hh=hh: kb.act(og[:, hh, tl], p_[:, :], AF.Sigmoid))
                cw_q = [PC[:, R_CM + j * 4 + pr: R_CM + j * 4 + pr + 1] for j in range(4)]
                cw_k = [PC[:, R_CM + j * 4 + 2 + pr: R_CM + j * 4 + 2 + pr + 1] for j in range(4)]
                for t in range(TB // 512):
                    tl = slice(t * 512, (t + 1) * 512)
                    p_ = ps()
                    kb.mm(p_[:, :], sel(2 + pr), bm[r16, tl])
                    kb.act(Abc[:, tl], p_[:, :], AF.Exp, scale=-1.0)
                    p2 = ps()
                    kb.mm(p2[:, :], sel(pr), T15[r16, tl], start=True, stop=False)
                    kb.mm(p2[:, :], sel(2 + pr), bm[r16, tl], start=False, stop=True)
                    kb.act(Cbc[:, tl], p2[:, :], AF.Exp)
                wq = wload("mqk0")
                wk = wload("mqk1")
                conv_silu_chunk(wq, pr * 128, halo_m[:, pr, :], cw_q, acc, hb, dg)
                kb.stt(qd, acc, sc, Abc, ALU.mult, ALU.mult)
                conv_silu_chunk(wk, pr * 128, halo_m[:, 2 + pr, :], cw_k, ks, hb, dg)
                kb.tt(kt, ks, Cbc, ALU.mult)
                kb.tt(kd.rearrange("p (c l) -> p c l", l=64), kt.rearrange("p (c l) -> p c l", l=64),
                      Abc.rearrange("p (c l) -> p c l", l=64)[:, :, 63:64].to_broadcast([128, NCH, 64]), ALU.mult)
                transpose_to_tok(kd_tok, kd)
                kb.copy(c32[:, 0, :], st_m[:, pr, :], eng="dve")
                for c in range(NCH):
                    tt = c // 2
                    cr = chunk_rows(c)
                    kb.copy(c_bf[:, c, :], c32[:, c % 2, :], eng="act")
                    pkv = ps()
                    o0 = 0
                    for hh in range(2):
                        hr_ = slice(hh * 64, hh * 64 + 64)
                        kb.mm(pkv[hr_, o0:o0 + 128], kd_tok[cr, tt, hh * 64:hh * 64 + 64],
                              v_tok[cr, tt, hh * 128:(hh + 1) * 128])
                        kb.mm(pkv[hr_, o0 + 128:o0 + 256], kd_tok[cr, tt, hh * 64:hh * 64 + 64], ones_b[cr, :])
                    kb.stt(c32[:, (c + 1) % 2, :], c32[:, c % 2, :], Abc[:, c * 64 + 63:c * 64 + 64],
                           pkv[:, o0:o0 + 256], ALU.mult, ALU.add)
                kb.copy(st_m[:, pr, :], c32[:, NCH % 2, :], eng="dve")
                for hh in range(2):
                    h = pr * 2 + hh
                    hr_ = slice(hh * 64, hh * 64 + 64)
                    for t in range(TB // 512):
                        tl = slice(t * 512, (t + 1) * 512)
                        attn_bd(at_sb, kt, qd, hr_, t)
                        po = ps()
                        pd = ps()
                        for t4 in range(4):
                            tt = t * 4 + t4
                            oc_ = slice(t4 * 128, (t4 + 1) * 128)
                            kb.mm(po[:, oc_], v_tok[:, tt, hh * 128:(hh + 1) * 128], at_sb[:, oc_], start=True, stop=False)
                            for cc in range(2):
                                c = tt * 2 + cc
                                kb.mm(po[:, t4 * 128 + cc * 64:t4 * 128 + cc * 64 + 64], c_bf[hr_, c, 0:128],
                                      qd[hr_, c * 64:(c + 1) * 64], start=False, stop=(cc == 1))
                            kb.mm(pd[:, oc_], ones_b, at_sb[:, oc_], start=True, stop=False)
                            for cc in range(2):
                                c = tt * 2 + cc
                                kb.mm(pd[:, t4 * 128 + cc * 64:t4 * 128 + cc * 64 + 64], c_bf[hr_, c, 128:256],
                                      qd[hr_, c * 64:(c + 1) * 64], start=False, stop=(cc == 1))
                        kb.act(ad, pd[:, :], AF.Abs)
                        kb.ts(ad, ad, 1.0, None, ALU.max)
                        kb.recip(ad, ad)
                        kb.tt(hm, po[:, :], ad, ALU.mult)
                        head_norm_out(hm, True, PC[:, R_MN + h:R_MN + h + 1], og[:, hh, tl], yT[:, h, tl])
            A.release()
            gdn(layer, blk, yT, Gg, Gtok, SG)
            A.release()
            mix3 = A.f32(8 * TB).rearrange("p (c t) -> p c t", c=8)

            def lhs(oc):
                if oc % 2 == 0:
                    lhs.w = wload("abwo%d" % (oc // 2))
                return [lhs.w[:, k, (oc % 2) * 128:(oc % 2) * 128 + 128] for k in range(8)]
            if dbg_d is not None:
                dv = dbg_d[3].rearrange("(c p) t -> p c t", p=128)
                for c in range(8):
                    kb.copy(mix3[:, c, :], yT[:, c, :], eng="dve")
                    kb.dma(dv[:, c, blk * TB:(blk + 1) * TB], mix3[:, c, :], eng="sp", is_out=True)
            out_proj_residual(layer, 1, yT, 8, lhs, mix3)
            A.release()

        def gdn(layer, blk, yT, Gg, Gtok, SG):
            A.mark()
            r16 = slice(0, 16)
            hb = A.bf(4 + TB)
            dg = A.bf(4 * 128).rearrange("p (j c) -> p j c", j=4)
            acc = A.f32(TB)
            kn = A.f32(TB)
            eG = A.f32(TB)
            eGd = A.f32(TB)
            beta = A.f32(TB)
            gg = A.bf(TB)
            qnb = A.bf(TB)
            qd = A.bf(TB)
            knb = A.bf(TB)
            kbb = A.bf(TB)
            kbG = A.bf(TB)
            kd = A.bf(TB)
            qp = kbb
            kd_tok = A.bf(TB).rearrange("p (a b) -> p a b", b=128)
            R_flat = A.bf(8 * 256)
            R_tok = R_flat.rearrange("p (a b) -> p a b", b=256)
            KcT = R_flat.rearrange("p (a b) -> p a b", b=128)
            X_tok = A.bf(8 * 256).rearrange("p (a b) -> p a b", b=256)
            scrA = A.bf(2 * TB)
            tmp_tok = scrA[:, 0:TB].rearrange("p (a b) -> p a b", b=128)
            vb = scrA[:, TB:2 * TB]
            s_bf = scrA.rearrange("p (a b) -> p a b", b=128)
            dtmp = acc[:, 512:1024]
            YT = dtmp
            DT = A.f32(512)
            DTs = A.f32(512)
            YTb = A.bf(512)
            Pm = A.bf(512)
            PTm = A.bf(512)
            P2 = A.bf(512)
            at_sb = A.bf(512)
            at2 = A.bf(8 * 128).rearrange("p (a b) -> p a b", b=128)
            D2 = acc[:, 0:512]
            s32 = A.f32(2 * 128).rearrange("p (a b) -> p a b", b=128)
            rq = DT
            kb.memset(at2, 0.0, eng="dve")
            sc = 128 ** -0.5
            wn = {"q": ("gq", 0), "k": ("gk", 4), "v": ("gv", 8)}

            for h in range(4):
                half, hh = h // 2, h % 2
                for t in range(TB // 512):
                    tl = slice(t * 512, (t + 1) * 512)
                    p_ = ps()
                    kb.mm(p_[:, :], sel(4 + h), Gg[r16, tl])
                    kb.copy(kn[:, tl], p_[:, :], eng="act")
                    kb.act(eG[:, tl], p_[:, :], AF.Exp)
                    p2 = ps()
                    kb.mm(p2[:, :], sel(8 + h), SG[r16, tl])
                    kb.copy(beta[:, tl], p2[:, :], eng="act")
                kb.tt(eGd.rearrange("p (c l) -> p c l", l=64),
                      kn.rearrange("p (c l) -> p c l", l=64)[:, :, 63:64].to_broadcast([128, NCH, 64]),
                      kn.rearrange("p (c l) -> p c l", l=64), ALU.subtract)
                kb.act(eGd, eGd, AF.Exp)
                wgg = wload("gg%d" % half)
                proj_fm(None, wgg, hh * 128, lambda p_, tl: kb.act(gg[:, tl], p_[:, :], AF.Silu))

                def conv_of(kind):
                    nm, cbase = wn[kind]
                    w = wload("%s%d" % (nm, half))
                    ch = cbase + h
                    wc = [PC[:, R_CG + j * 12 + ch: R_CG + j * 12 + ch + 1] for j in range(4)]
                    conv_silu_chunk(w, hh * 128, halo_g[:, ch, :], wc, acc, hb, dg)

                def l2rstd(t):
                    tl = slice(t * 512, (t + 1) * 512)
                    kb.act(sqb[:, 0, :], acc[:, tl], AF.Square)
                    p_ = ps()
                    kb.mm(p_[:, :], ones_b, sqb[:, 0, :])
                    kb.act(lnv[:, :], p_[:, :], AF.Ln, bias=eps_c, scale=1.0)
                    kb.act(rq[:, :], lnv[:, :], AF.Exp, scale=-0.5)
                conv_of("q")
                for t in range(TB // 512):
                    tl = slice(t * 512, (t + 1) * 512)
                    l2rstd(t)
                    kb.stt(acc[:, tl], acc[:, tl], sc, rq, ALU.mult, ALU.mult)
                kb.copy(qnb, acc, eng="pool")
                kb.tt(qd, acc, eG, ALU.mult)
                conv_of("k")
                for t in range(TB // 512):
                    tl = slice(t * 512, (t + 1) * 512)
                    l2rstd(t)
                    kb.tt(kn[:, tl], acc[:, tl], rq, ALU.mult)
                kb.copy(knb, kn, eng="pool")
                kb.tt(kd, kn, eGd, ALU.mult)
                kb.tt(kn, kn, beta, ALU.mult)
                kb.copy(kbb, kn, eng="pool")
                kb.tt(kbG, kn, eG, ALU.mult)
                conv_of("v")
                kb.tt(vb, acc, beta, ALU.mult)
                pg = ps()
                for tt in range(8):
                    for cc in range(2):
                        c = tt * 2 + cc
                        kb.mm(pg[cc * 64:(cc + 1) * 64, tt * 64:(tt + 1) * 64], sel(4 + h)[:, cc * 64:(cc + 1) * 64],
                              Gg[r16, c * 64:(c + 1) * 64])
                for tt in range(8):
                    kb.stt(dtmp[:, tt * 64:(tt + 1) * 64], pg[:, tt * 64:(tt + 1) * 64],
                           Gtok[:, tt, 8 + h:9 + h], negm[:, 0:64], ALU.subtract, ALU.add)
                kb.act(DT, dtmp, AF.Exp)
                kb.tt(DTs, DT, m_su, ALU.mult)
                for tt in range(8):
                    kb.stt(dtmp[:, tt * 64:(tt + 1) * 64], pg[:, tt * 64:(tt + 1) * 64],
                           Gtok[:, tt, 8 + h:9 + h], posm[:, 0:64], ALU.subtract, ALU.add)
                kb.act(D2, dtmp, AF.Exp, scale=-1.0)
                transpose_to_tok(kd_tok, kd)
                transpose_to_tok(tmp_tok, vb)
                kb.copy(R_tok[:, :, 0:128], tmp_tok, eng="pool")
                transpose_to_tok(tmp_tok, kbG)
                kb.copy(R_tok[:, :, 128:256], tmp_tok, eng="pool")
                for t in range(TB // 512):
                    pa = ps()
                    pb = ps()
                    pc_ = ps()
                    for c8 in range(8):
                        c = t * 8 + c8
                        cs = slice(c * 64, (c + 1) * 64)
                        o_ = slice((c8 // 2) * 64, (c8 // 2) * 64 + 64)
                        kb.mm(pa[chunk_rows(c), o_], knb[:, cs], kbb[:, cs])
                        kb.mm(pb[chunk_rows(c), o_], kbb[:, cs], knb[:, cs])
                        kb.mm(pc_[chunk_rows(c), o_], knb[:, cs], qnb[:, cs])
                    tw = slice(t * 256, (t + 1) * 256)
                    kb.tt(PTm[:, tw], pa[:, 0:256], DTs[:, tw], ALU.mult)
                    kb.tt(Pm[:, tw], pb[:, 0:256], D2[:, tw], ALU.mult)
                    kb.tt(at_sb[:, tw], pc_[:, 0:256], DT[:, tw], ALU.mult)
                W512 = slice(0, 512)
                lo, hi = slice(0, 64), slice(64, 128)
                kb.copy(at2[lo, :, 0:64], at_sb[lo, :].rearrange("p (a b) -> p a b", b=64), eng="pool")
                kb.copy(at2[hi, :, 64:128], at_sb[hi, :].rearrange("p (a b) -> p a b", b=64), eng="pool")
                kb.tt(YT[:, W512], iblk, PTm[:, W512], ALU.subtract)
                kb.copy(YTb[:, W512], YT[:, W512], eng="act")

                def blockmm(dst_pair, lhs, rhs):
                    for c in range(NCH):
                        cr = chunk_rows(c)
                        o_ = slice((c // 2) * 64, (c // 2) * 64 + 64)
                        kb.mm(dst_pair[c % 2][cr, o_], lhs[cr, o_], rhs[cr, o_])
                for step in range(5):
                    pa2 = [ps(), ps()]
                    blockmm(pa2, PTm, Pm)
                    pb2 = [ps(), ps()]
                    blockmm(pb2, Pm, PTm)
                    kb.copy(P2[lo, W512], pa2[0][lo, :], eng="act")
                    kb.copy(P2[hi, W512], pa2[1][hi, :], eng="act")
                    kb.copy(PTm[lo, W512], pb2[0][lo, :], eng="dve")
                    kb.copy(PTm[hi, W512], pb2[1][hi, :], eng="dve")
                    kb.copy(Pm[:, W512], P2[:, W512], eng="pool")
                    pc2 = [ps(), ps()]
                    blockmm(pc2, P2, YTb)
                    kb.tt(YT[lo, W512], YT[lo, W512], pc2[0][lo, :], ALU.add)
                    kb.tt(YT[hi, W512], YT[hi, W512], pc2[1][hi, :], ALU.add)
                    if step < 4:
                        kb.copy(YTb[:, W512], YT[:, W512], eng="act")
                kb.tt(YTb[:, W512], YT[:, W512], iblk, ALU.subtract)
                for tt in range(8):
                    if tt % 2 == 0:
                        px2 = [ps(), ps()]
                    for cc in range(2):
                        c = tt * 2 + cc
                        cr = chunk_rows(c)
                        o_ = slice((c // 2) * 64, (c // 2) * 64 + 64)
                        kb.mm(px2[cc][cr, (tt % 2) * 256:(tt % 2) * 256 + 256], YTb[cr, o_], R_tok[cr, tt, :])
                    if tt % 2 == 1:
                        for cc, hs in ((0, lo), (1, hi)):
                            kb.tt(X_tok[hs, tt - 1:tt + 1, :], px2[cc][hs, :].rearrange("p (a b) -> p a b", b=256),
                                  R_tok[hs, tt - 1:tt + 1, :], ALU.add)
                for t in range(TB // 512):
                    tl = slice(t * 512, (t + 1) * 512)
                    pq = ps()
                    for t4 in range(4):
                        tt = t * 4 + t4
                        kb.mm(pq[:, t4 * 128:(t4 + 1) * 128], X_tok[:, tt, 128:256], at2[:, tt, :])
                    kb.tt(qp[:, tl], qd[:, tl], pq[:, :], ALU.subtract)
                KcT4 = R_flat.rearrange("p (a two b) -> p a two b", two=2, b=128)
                for g in range(2):
                    pk2 = [ps(), ps()]
                    for c8 in range(8):
                        c = g * 8 + c8
                        cr = chunk_rows(c)
                        kb.mm(pk2[c % 2][:, (c8 // 2) * 128:(c8 // 2 + 1) * 128], X_tok[cr, c // 2, 128:256],
                              kd_tok[cr, c // 2, :])
                    for cc in range(2):
                        kb.act(KcT4[:, g * 4:(g + 1) * 4, cc, :], pk2[cc][:, :].rearrange("p (a b) -> p a b", b=128),
                               AF.Copy, scale=-1.0)
                kb.copy(s32[:, 0, :], st_g[:, h, :], eng="dve")
                for c in range(NCH):
                    cr = chunk_rows(c)
                    kb.copy(s_bf[:, c, :], s32[:, c % 2, :], eng="act")
                    pss = ps()
                    o_ = slice(0, 128)
                    kb.mm(pss[:, o_], kd_tok[cr, c // 2, :], X_tok[cr, c // 2, 0:128], start=True, stop=False)
                    kb.mm(pss[:, o_], KcT[:, c, :], s_bf[:, c, :], start=False, stop=True)
                    kb.stt(s32[:, (c + 1) % 2, :], s32[:, c % 2, :], eG[:, c * 64 + 63:c * 64 + 64],
                           pss[:, o_], ALU.mult, ALU.add)
                kb.copy(st_g[:, h, :], s32[:, NCH % 2, :], eng="dve")
                for t in range(TB // 512):
                    tl = slice(t * 512, (t + 1) * 512)
                    po = ps()
                    for t4 in range(4):
                        tt = t * 4 + t4
                        oc_ = slice(t4 * 128, (t4 + 1) * 128)
                        kb.mm(po[:, oc_], X_tok[:, tt, 0:128], at2[:, tt, :], start=True, stop=False)
                        for cc in range(2):
                            c = tt * 2 + cc
                            kb.mm(po[:, t4 * 128 + cc * 64:t4 * 128 + cc * 64 + 64], s_bf[:, c, :],
                                  qp[:, c * 64:(c + 1) * 64], start=False, stop=(cc == 1))
                    head_norm_out(po[:, :], False, PC[:, R_GN + h:R_GN + h + 1], gg[:, tl], yT[:, 4 + h, tl])
            A.release()

        xv = xT_d.rearrange("(c p) t -> p c t", p=128)
        ov = oT_d.rearrange("(c p) t -> p c t", p=128)

        def checkpoint(i, blk):
            if dbg_d is not None:
                dv = dbg_d[i].rearrange("(c p) t -> p c t", p=128)
                for c in range(8):
                    kb.dma(dv[:, c, blk * TB:(blk + 1) * TB], xT[:, c, :], eng="sp", is_out=True)

        for blk in range(NBLK):
            for c in range(8):
                kb.dma(xT[:, c, :], xv[:, c, blk * TB:(blk + 1) * TB])
            if "mix0" not in (stop_after or ()):
                ab_mixer(0, blk)
            checkpoint(0, blk)
            if "ffn0" not in (stop_after or ()):
                ffn(0)
            checkpoint(1, blk)
            if n_layers > 1:
                if "mix1" not in (stop_after or ()):
                    hgrn2(1, blk)
                checkpoint(2, blk)
                if "ffn1" not in (stop_after or ()):
                    ffn(1)
            for c in range(8):
                kb.dma(ov[:, c, blk * TB:(blk + 1) * TB], xT[:, c, :], eng="sp", is_out=True)
        kb.P.finalize()
        kb.P.emit(nc, block, sems)
        print("program ops:", len(kb.P.ops), "arena top:", A.top, flush=True)
    return nc


_CACHE = {}


def _host_inputs(inputs):
    WP = pack_weights(inputs)
    pv, gsc = pack_params(inputs)
    cb, cf = make_consts()
    return WP, pv, gsc, cb, cf


def kernel(**inputs):
    inputs = {k: np.asarray(v) for k, v in inputs.items()}
    x = inputs["x"].astype(np.float32, copy=False)
    WP, pv, gsc, cb, cf = _host_inputs(inputs)
    if "nc" not in _CACHE:
        _CACHE["nc"] = build_program()
    nc = _CACHE["nc"]
    in_maps = []
    for b in range(8):
        in_maps.append({"xT": np.ascontiguousarray(x[b].T), "wp": WP, "pv": pv, "gsc": gsc, "cb": cb, "cf": cf})
    res = run_bass_kernel_spmd(nc, in_maps, core_ids=list(range(8)))
    out = np.stack([np.ascontiguousarray(res.results[b]["oT"].T) for b in range(8)], axis=0)
    return out.astype(np.float32)
```

```python
import numpy as np
import concourse.bass as bass
import concourse.mybir as mybir
from concourse.bass_utils import run_bass_kernel_spmd

F32 = mybir.dt.float32
BF16 = mybir.dt.bfloat16
AF = mybir.ActivationFunctionType
ALU = mybir.AluOpType
DSZ = {F32: 4, BF16: 2}

N_DMA_SEMS = 24


def _box(ap):
    t = ap.tensor
    kind = type(t).__name__
    if kind.startswith("DRam"):
        return None
    if kind.startswith("PSum"):
        return (t.name, 0, 128, 0, 1 << 30)
    row = 1
    for s in list(t.shape)[1:]:
        row *= int(s)
    dsz = DSZ[ap.dtype] if ap.dtype in DSZ else 4
    tds = DSZ.get(t.dtype, 4)
    off = int(ap.offset)
    p0 = off // row
    f0 = off % row
    aps = [(int(s), int(c)) for (s, c) in ap.ap]
    pc = aps[0][1]
    ext = 1
    for (s, c) in aps[1:]:
        ext += abs(s) * (c - 1)
    return (t.name, p0, p0 + pc, f0 * tds, (f0 + ext) * tds)


class _Op:
    __slots__ = ("eng", "fn", "deps", "waits", "signal", "token", "dma", "id", "presem")

    def __init__(self, eng, fn, dma):
        self.eng = eng
        self.fn = fn
        self.dma = dma
        self.deps = set()
        self.waits = []
        self.signal = False
        self.token = None
        self.presem = None


class Prog:
    ENGS = ("pe", "act", "dve", "pool", "sp")

    def __init__(self):
        self.ops = []
        self.hist = {}
        self.out_dma_ops = []

    def add(self, eng, fn, reads=(), writes=(), dma=False, is_out=False):
        op = _Op(eng, fn, dma)
        op.id = len(self.ops)
        self.ops.append(op)
        rb = [b for b in (_box(a) for a in reads) if b is not None]
        wb = [b for b in (_box(a) for a in writes) if b is not None]
        for (name, p0, p1, f0, f1) in rb:
            for h in self.hist.get(name, ()):
                if h[5] and h[0] < p1 and p0 < h[1] and h[2] < f1 and f0 < h[3]:
                    op.deps.add(h[4])
        for (name, p0, p1, f0, f1) in wb:
            for h in self.hist.get(name, ()):
                if h[0] < p1 and p0 < h[1] and h[2] < f1 and f0 < h[3]:
                    op.deps.add(h[4])
        op.deps.discard(op.id)
        for (name, p0, p1, f0, f1) in wb:
            lst = self.hist.setdefault(name, [])
            lst[:] = [h for h in lst if not (p0 <= h[0] and h[1] <= p1 and f0 <= h[2] and h[3] <= f1)]
            lst.append((p0, p1, f0, f1, op.id, True, eng))
        for (name, p0, p1, f0, f1) in rb:
            lst = self.hist.setdefault(name, [])
            lst[:] = [h for h in lst if not ((not h[5]) and h[6] == eng and (not dma)
                                             and p0 <= h[0] and h[1] <= p1 and f0 <= h[2] and h[3] <= f1
                                             and not self.ops[h[4]].dma)]
            lst.append((p0, p1, f0, f1, op.id, False, eng))
        if is_out:
            self.out_dma_ops.append(op.id)
        return op

    def finalize(self):
        ops = self.ops
        for op in ops:
            for d in op.deps:
                p = ops[d]
                if p.dma:
                    continue
                if p.eng == op.eng and p.eng == "pe" and not op.dma:
                    continue
                p.signal = True
        counts = {e: 0 for e in self.ENGS}
        dma_counts = [0] * N_DMA_SEMS
        dma_rr = 0
        waited = {e: {} for e in self.ENGS}
        for op in ops:
            w = {}
            if op.dma:
                s = dma_rr % N_DMA_SEMS
                dma_rr += 1
                key = ("dma", s)
                if dma_counts[s] > 0:
                    w[key] = dma_counts[s]
                dma_counts[s] += 16
                op.token = (key, dma_counts[s])
            elif op.signal:
                counts[op.eng] += 1
                op.token = (("eng", op.eng), counts[op.eng])
            for d in sorted(op.deps):
                p = ops[d]
                if (not p.dma) and (not op.dma) and p.eng == op.eng and p.eng == "pe":
                    continue
                key, val = p.token
                if w.get(key, 0) < val:
                    w[key] = val
            wd = waited[op.eng]
            op.waits = []
            for key, val in w.items():
                if wd.get(key, 0) < val:
                    wd[key] = val
                    op.waits.append((key, val))
        self.final_waits = {}
        for oid in self.out_dma_ops:
            key, val = ops[oid].token
            if self.final_waits.get(key, 0) < val:
                self.final_waits[key] = val

    def emit(self, nc, block, sems):
        per = {e: [op for op in self.ops if op.eng == e] for e in self.ENGS}
        final_waits = self.final_waits

        def body(engname):
            def _f(eng):
                for op in per[engname]:
                    for key, val in op.waits:
                        eng.wait_ge(sems[key], val)
                    inst = op.fn(eng)
                    if op.dma:
                        inst.then_inc(sems[op.token[0]], 16)
                    elif op.signal:
                        inst.then_inc(sems[op.token[0]], 1)
                if engname == "sp":
                    for key, val in final_waits.items():
                        eng.wait_ge(sems[key], val)
            return _f

        block.tensor(body("pe"))
        block.scalar(body("act"))
        block.vector(body("dve"))
        block.gpsimd(body("pool"))
        block.sync(body("sp"))


class KB:
    def __init__(self, nc):
        self.nc = nc
        self.P = Prog()

    @staticmethod
    def _aps(*xs):
        return [x for x in xs if x is not None and not isinstance(x, (int, float))]

    def mm(self, out, lhsT, rhs, start=True, stop=True):
        self.P.add("pe", lambda e: e.matmul(out, lhsT, rhs, start=start, stop=stop),
                   reads=[lhsT, rhs], writes=[out])

    def transpose(self, out, in_, ident):
        self.P.add("pe", lambda e: e.transpose(out, in_, ident), reads=[in_, ident], writes=[out])

    def act(self, out, in_, func, bias=None, scale=None, eng="act"):
        kw = {}
        if bias is not None:
            kw["bias"] = bias
        if scale is not None:
            kw["scale"] = scale
        self.P.add(eng, lambda e: e.activation(out, in_, func, **kw),
                   reads=self._aps(in_, bias, scale), writes=[out])

    def tt(self, out, in0, in1, op, eng="dve"):
        self.P.add(eng, lambda e: e.tensor_tensor(out, in0, in1, op), reads=[in0, in1], writes=[out])

    def ts(self, out, in0, s1, s2, op0, op1=None, eng="dve"):
        if op1 is None:
            self.P.add(eng, lambda e: e.tensor_scalar(out, in0, s1, None, op0),
                       reads=self._aps(in0, s1), writes=[out])
        else:
            self.P.add(eng, lambda e: e.tensor_scalar(out, in0, s1, s2, op0, op1),
                       reads=self._aps(in0, s1, s2), writes=[out])

    def stt(self, out, in0, scalar, in1, op0, op1):
        self.P.add("dve", lambda e: e.scalar_tensor_tensor(out, in0, scalar, in1, op0, op1),
                   reads=self._aps(in0, scalar, in1), writes=[out])

    def scan(self, out, d0, d1, initial, op0, op1):
        self.P.add("dve", lambda e: e.tensor_tensor_scan(out, d0, d1, initial, op0, op1),
                   reads=self._aps(d0, d1, initial), writes=[out])

    def copy(self, out, in_, eng="dve"):
        if eng == "act":
            self.P.add(eng, lambda e: e.copy(out, in_), reads=[in_], writes=[out])
        else:
            self.P.add(eng, lambda e: e.tensor_copy(out, in_), reads=[in_], writes=[out])

    def recip(self, out, in_):
        self.P.add("dve", lambda e: e.reciprocal(out, in_), reads=[in_], writes=[out])

    def memset(self, ap, val, eng="pool"):
        self.P.add(eng, lambda e: e.memset(ap, val), reads=[], writes=[ap])

    def dma(self, out, in_, eng="sp", is_out=False):
        self.P.add(eng, lambda e: e.dma_start(out=out, in_=in_), reads=[in_], writes=[out], dma=True,
                   is_out=is_out)


D = 1024
S = 2048
TB = 1024
NBLK = S // TB
NCH = TB // 64
EPS = 1e-6
DFF = 2816
NPAIR = DFF // 128

R_NG, R_FCW, R_FCB, R_CM, R_CG, R_MN, R_GN, R_LB, R_CN = 0, 64, 328, 416, 432, 480, 484, 488, 504

CB_ID, CB_ONE, CB_MI, CB_MSU, CB_MSL, CB_IB, CB_SCAN, CB_NEG, CB_POS, CB_M2, CB_N = 0, 128, 256, 768, 1280, 1792, 2304, 3328, 3392, 3456, 3968
CF_ID, CF_SEL, CF_N = 0, 128, 128 + 12 * 128


def unit_table():
    tab = {}
    idx = 0

    def add(name, nk, nc_):
        nonlocal idx
        tab[name] = (idx, nk, nc_)
        idx += 1
    for n in ("mqk0", "mqk1", "mv0", "mv1", "mo0", "mo1", "gq0", "gq1", "gk0", "gk1", "gv0", "gv1",
              "gg0", "gg1"):
        add(n, 8, 256)
    add("gates", 8, 16)
    for i in range(4):
        add("abwo%d" % i, 8, 256)
    for h in range(8):
        add("cqf%d" % h, 8, 256)
        add("cig%d" % h, 8, 256)
    for i in range(4):
        add("cwo%d" % i, 8, 256)
    for l in range(2):
        for j in range(NPAIR):
            add("up%d_%d" % (l, j), 8, 256)
        for oc in range(8):
            for kh in range(2):
                add("dn%d_%d_%d" % (l, oc, kh), 11, 128)
    return tab, idx


def _pack(W, ks, cols):
    Wr = W.reshape(W.shape[0] // 128, 128, W.shape[1])
    return np.ascontiguousarray(np.transpose(Wr[ks][:, :, cols], (1, 0, 2)))


def pack_weights(inp):
    tab, n = unit_table()
    WP = np.zeros((n, 128, 2048), np.float32)

    def put(name, arr):
        i, nk, nc_ = tab[name]
        assert arr.shape == (128, nk, nc_), (name, arr.shape)
        WP[i, :, : nk * nc_] = arr.reshape(128, nk * nc_)
    k8 = list(range(8))
    w = inp["ab_w_in"][0]
    base = {"mqk": 0, "mv": 512, "mo": 1024, "gq": 1544, "gk": 2056, "gv": 2568, "gg": 3080}
    for nm, b in base.items():
        for i in range(2):
            put("%s%d" % (nm, i), _pack(w, k8, np.arange(b + i * 256, b + (i + 1) * 256)))
    put("gates", _pack(w, k8, np.concatenate([np.arange(1536, 1544), np.arange(3592, 3600)])))
    for i in range(4):
        put("abwo%d" % i, _pack(inp["ab_w_out"][0], k8, np.arange(i * 256, (i + 1) * 256)))
        put("cwo%d" % i, _pack(inp["c_w_out"][0], k8, np.arange(i * 256, (i + 1) * 256)))
    cw = inp["c_w_in"][0]
    for h in range(8):
        put("cqf%d" % h, _pack(cw, k8, np.concatenate([np.arange(h * 128, (h + 1) * 128),
                                                        np.arange(1024 + h * 128, 1024 + (h + 1) * 128)])))
        put("cig%d" % h, _pack(cw, k8, np.concatenate([np.arange(2048 + h * 128, 2048 + (h + 1) * 128),
                                                        np.arange(3072 + h * 128, 3072 + (h + 1) * 128)])))
    for l in range(2):
        wu = inp["ffn_w_up"][l]
        wd = inp["ffn_w_down"][l]
        for j in range(NPAIR):
            put("up%d_%d" % (l, j), _pack(wu, k8, np.concatenate([np.arange(j * 128, (j + 1) * 128),
                                                                   np.arange(DFF + j * 128, DFF + (j + 1) * 128)])))
        for oc in range(8):
            for kh in range(2):
                put("dn%d_%d_%d" % (l, oc, kh), _pack(wd, list(range(kh * 11, kh * 11 + 11)),
                                                      np.arange(oc * 128, (oc + 1) * 128)))
    return WP


def pack_params(inp):
    pv = np.zeros((512, 128), np.float32)
    pv[R_NG:R_NG + 64] = inp["norm_gains"].reshape(64, 128)
    pv[R_FCW:R_FCW + 264] = inp["ffn_conv_w"].reshape(2 * 3 * 44, 128)
    pv[R_FCB:R_FCB + 88] = inp["ffn_conv_b"].reshape(88, 128)
    pv[R_CM:R_CM + 16] = inp["ab_conv_m"][0].reshape(16, 128)
    pv[R_CG:R_CG + 48] = inp["ab_conv_g"][0].reshape(48, 128)
    pv[R_MN:R_MN + 4] = inp["ab_m_norm"][0].reshape(4, 128)
    pv[R_GN:R_GN + 4] = inp["ab_g_norm"][0].reshape(4, 128)
    pv[R_LB:R_LB + 16] = inp["c_lb_logits"].reshape(16, 128)
    pv[R_CN:R_CN + 8] = inp["c_norm"][0].reshape(8, 128)
    gsc = np.zeros((16, 2), np.float32)
    gsc[0:4, 0] = inp["ab_m_gate_bias"][0, 0]
    gsc[4:8, 0] = inp["ab_m_gate_bias"][0, 1]
    gsc[8:12, 0] = inp["ab_g_dt_bias"][0]
    gsc[8:12, 1] = inp["ab_g_a_log"][0]
    return pv, gsc


def make_consts():
    import ml_dtypes
    p = np.arange(128)[:, None]
    cb = np.zeros((128, CB_N), np.float32)
    cb[:, CB_ID:CB_ID + 128] = np.eye(128)
    cb[:, CB_ONE:CB_ONE + 128] = 1.0
    col = np.arange(512)[None, :]
    l = col % 64
    cb[:, CB_MI:CB_MI + 512] = ((p % 64) <= l)
    cb[:, CB_MSU:CB_MSU + 512] = ((p % 64) < l)
    cb[:, CB_MSL:CB_MSL + 512] = (l < (p % 64))
    cb[:, CB_IB:CB_IB + 512] = ((p % 64) == l)
    t = np.arange(1024)[None, :]
    cb[:, CB_SCAN:CB_SCAN + 1024] = np.broadcast_to((t % 64) != 0, (128, 1024))
    cb[:, CB_NEG:CB_NEG + 64] = np.where((p % 64) > np.arange(64)[None, :], -1e30, 0.0)
    cb[:, CB_POS:CB_POS + 64] = np.where(np.arange(64)[None, :] >= (p % 64), 1e30, 0.0)
    j = np.arange(512)[None, :]
    cb[:, CB_M2:CB_M2 + 512] = ((p // 64) == ((j % 128) // 64)) & ((p % 64) <= (j % 64))
    cf = np.zeros((128, CF_N), np.float32)
    cf[:, CF_ID:CF_ID + 128] = np.eye(128)
    sel = np.zeros((12, 16, 128), np.float32)
    for pr in range(2):
        sel[pr, 2 * pr, 0:64] = 1.0
        sel[pr, 2 * pr + 1, 64:128] = 1.0
        sel[2 + pr, 4 + 2 * pr, 0:64] = 1.0
        sel[2 + pr, 4 + 2 * pr + 1, 64:128] = 1.0
    for h in range(4):
        sel[4 + h, 8 + h, :] = 1.0
        sel[8 + h, 12 + h, :] = 1.0
    for i in range(12):
        cf[0:16, CF_SEL + i * 128: CF_SEL + (i + 1) * 128] = sel[i]
    return cb.astype(ml_dtypes.bfloat16), cf


SB_BYTES = 212000


class Arena:
    def __init__(self, SB):
        self.SB = SB
        self.top = 0
        self.marks = []

    def alloc(self, nbytes):
        nbytes = (nbytes + 63) // 64 * 64
        off = self.top
        self.top += nbytes
        assert self.top <= SB_BYTES, ("SBUF arena overflow", self.top)
        return off

    def f32(self, n):
        off = self.alloc(4 * n)
        return self.SB[:, off // 2: off // 2 + 2 * n].bitcast(F32)

    def bf(self, n):
        off = self.alloc(2 * n)
        return self.SB[:, off // 2: off // 2 + n]

    def mark(self):
        self.marks.append(self.top)

    def release(self):
        self.top = self.marks.pop()


def build_program(n_layers=2, dbg=False, stop_after=None):
    import contextlib
    nc = bass.Bass("TRN2", target_bir_lowering=False)
    tab, nunits = unit_table()
    xT_d = nc.dram_tensor("xT", [D, S], F32, kind="ExternalInput").ap()
    wp_d = nc.dram_tensor("wp", [nunits, 128, 2048], F32, kind="ExternalInput").ap()
    pv_d = nc.dram_tensor("pv", [512, 128], F32, kind="ExternalInput").ap()
    gsc_d = nc.dram_tensor("gsc", [16, 2], F32, kind="ExternalInput").ap()
    cb_d = nc.dram_tensor("cb", [128, CB_N], BF16, kind="ExternalInput").ap()
    cf_d = nc.dram_tensor("cf", [128, CF_N], F32, kind="ExternalInput").ap()
    oT_d = nc.dram_tensor("oT", [D, S], F32, kind="ExternalOutput").ap()
    dbg_d = None
    if dbg:
        dbg_d = nc.dram_tensor("dbg", [4, D, S], F32, kind="ExternalOutput").ap()

    with contextlib.ExitStack() as st:
        SB = st.enter_context(nc.sbuf_tensor("SB", [128, SB_BYTES // 2], BF16))
        PS = [st.enter_context(nc.psum_tensor("ps%d" % i, [128, 512], F32)) for i in range(7)]
        PT = st.enter_context(nc.psum_tensor("pst", [128, 1024], BF16))
        sems = {}
        for e in Prog.ENGS:
            sems[("eng", e)] = st.enter_context(nc.semaphore("s_" + e))
        for i in range(N_DMA_SEMS):
            sems[("dma", i)] = st.enter_context(nc.semaphore("d%d" % i))
        block = st.enter_context(nc.Block())
        kb = KB(nc)
        A = Arena(SB)
        psi = [0]

        def ps():
            psi[0] = (psi[0] + 1) % 7
            return PS[psi[0]]

        xT = A.f32(8 * TB).rearrange("p (c t) -> p c t", c=8)
        PC = A.f32(512)
        cb = A.bf(CB_N)
        cf = A.f32(CF_N)
        misc = A.f32(64)
        gsc = A.f32(2)
        st_h = A.f32(8 * 128).rearrange("p (h v) -> p h v", h=8)
        st_m = A.f32(2 * 256).rearrange("p (h v) -> p h v", h=2)
        st_g = A.f32(4 * 128).rearrange("p (h v) -> p h v", h=4)
        halo_f = A.f32(2 * 44 * 2).rearrange("p (l c j) -> p l c j", l=2, c=44)
        halo_m = A.f32(4 * 3).rearrange("p (c j) -> p c j", c=4)
        halo_g = A.f32(12 * 3).rearrange("p (c j) -> p c j", c=12)
        wstage = [A.f32(2048) for _ in range(2)]
        wbf = [A.bf(2048) for _ in range(4)]
        uT = A.bf(8 * TB)
        uT3 = uT.rearrange("p (c t) -> p c t", c=8)
        sqb = A.bf(8 * 512).rearrange("p (c t) -> p c t", c=8)
        rstd = A.f32(512)
        lnv = A.f32(512)

        ident_f = cf[:, CF_ID:CF_ID + 128]
        ident_b = cb[:, CB_ID:CB_ID + 128]
        ones_b = cb[:, CB_ONE:CB_ONE + 128]
        m_incl = cb[:, CB_MI:CB_MI + 512]
        m_su = cb[:, CB_MSU:CB_MSU + 512]
        m_sl = cb[:, CB_MSL:CB_MSL + 512]
        iblk = cb[:, CB_IB:CB_IB + 512]
        scanm = cb[:, CB_SCAN:CB_SCAN + 1024]
        negm = cb[:, CB_NEG:CB_NEG + 64]
        posm = cb[:, CB_POS:CB_POS + 64]
        m2 = cb[:, CB_M2:CB_M2 + 512]
        eps_c = misc[:, 0:1]
        one_c = misc[:, 1:2]
        lbc = misc[:, 8:16]
        omlc = misc[:, 16:24]
        nac = misc[:, 24:25]

        def sel(i):
            return cf[0:16, CF_SEL + i * 128: CF_SEL + (i + 1) * 128]

        import os
        LVL = int(os.environ.get("SETUP_LVL", "99"))
        kb.dma(cb, cb_d)
        kb.dma(cf, cf_d)
        if LVL >= 1:
            kb.dma(gsc[0:16, :], gsc_d)
        kb.memset(misc, 0.0, eng="dve")
        kb.memset(eps_c, EPS, eng="dve")
        kb.memset(one_c, 1.0, eng="dve")
        for t_ in (st_h, st_m, st_g, halo_f, halo_m, halo_g):
            kb.memset(t_, 0.0, eng="dve")
        pstg = wstage[0]
        for i in range(4):
            kb.dma(pstg[:, i * 128:(i + 1) * 128], pv_d[i * 128:(i + 1) * 128, :])
        for i in range(4 if LVL >= 2 else 0):
            p_ = ps()
            kb.transpose(p_[:, 0:128], pstg[:, i * 128:(i + 1) * 128], ident_f)
            kb.copy(PC[:, i * 128:(i + 1) * 128], p_[:, 0:128], eng="act")
        if LVL >= 3:
            kb.tt(lbc, PC[:, R_LB + 8:R_LB + 16], PC[:, R_LB:R_LB + 8], ALU.subtract)
            kb.act(lbc, lbc, AF.Sigmoid)
            kb.ts(omlc, lbc, -1.0, 1.0, ALU.mult, ALU.add)
        if LVL >= 4:
            kb.act(nac[0:16, :], gsc[0:16, 1:2], AF.Exp)
            kb.ts(nac[0:16, :], nac[0:16, :], -1.0, None, ALU.mult)

        wcnt = [0]

        def wload(name):
            i, nk, ncl = tab[name]
            n = nk * ncl
            k = wcnt[0]
            wcnt[0] += 1
            stg = wstage[k % 2]
            dst = wbf[k % 4]
            kb.dma(stg[:, 0:n], wp_d[i, :, 0:n])
            kb.copy(dst[:, 0:n], stg[:, 0:n], eng="pool")
            return dst[:, 0:n].rearrange("p (k c) -> p k c", k=nk)

        def gcol(layer, idx, c):
            r = R_NG + (layer * 4 + idx) * 8 + c
            return PC[:, r:r + 1]

        def rstd_from_sum(p_sum, inv_n, ncols=512):
            kb.act(lnv[:, 0:ncols], p_sum[:, 0:ncols], AF.Ln, bias=eps_c, scale=inv_n)
            kb.act(rstd[:, 0:ncols], lnv[:, 0:ncols], AF.Exp, scale=-0.5)

        def prenorm(layer, idx):
            for t in range(TB // 512):
                tl = slice(t * 512, (t + 1) * 512)
                for c in range(8):
                    kb.act(sqb[:, c, :], xT[:, c, tl], AF.Square)
                p_ = ps()
                for c in range(8):
                    kb.mm(p_[:, :], ones_b, sqb[:, c, :], start=(c == 0), stop=(c == 7))
                rstd_from_sum(p_, 1.0 / D)
                for c in range(8):
                    kb.stt(uT3[:, c, tl], xT[:, c, tl], gcol(layer, idx, c), rstd[:, :], ALU.mult, ALU.mult)

        def out_proj_residual(layer, idx, src3, nk, lhs_fn, mix3):
            for oc in range(8):
                lhs = lhs_fn(oc)
                for t in range(TB // 512):
                    tl = slice(t * 512, (t + 1) * 512)
                    p_ = ps()
                    for k in range(nk):
                        kb.mm(p_[:, :], lhs[k], src3[:, k, tl], start=(k == 0), stop=(k == nk - 1))
                    kb.act(mix3[:, oc, tl], p_[:, :], AF.Copy)
                    kb.act(uT3[:, oc, tl], p_[:, :], AF.Square)
            for t in range(TB // 512):
                tl = slice(t * 512, (t + 1) * 512)
                p2 = ps()
                for c in range(8):
                    kb.mm(p2[:, :], ones_b, uT3[:, c, tl], start=(c == 0), stop=(c == 7))
                rstd_from_sum(p2, 1.0 / D)
                for c in range(8):
                    kb.stt(mix3[:, c, tl], mix3[:, c, tl], gcol(layer, idx, c), rstd[:, :], ALU.mult, ALU.mult)
                    kb.tt(xT[:, c, tl], xT[:, c, tl], mix3[:, c, tl], ALU.add, eng="pool")

        def conv_taps(acc, hraw, K, wcols, bias_col):
            if bias_col is not None:
                kb.ts(acc, hraw[:, K - 1:K - 1 + TB], wcols[K - 1], bias_col, ALU.mult, ALU.add)
            else:
                kb.ts(acc, hraw[:, K - 1:K - 1 + TB], wcols[K - 1], None, ALU.mult)
            for j in range(K - 1):
                kb.stt(acc, hraw[:, j:j + TB], wcols[j], acc, ALU.mult, ALU.add)

        def proj_fm(dst_fn, wtile, col0, evac):
            for t in range(TB // 512):
                tl = slice(t * 512, (t + 1) * 512)
                p_ = ps()
                for k in range(8):
                    kb.mm(p_[:, :], wtile[:, k, col0:col0 + 128], uT3[:, k, tl], start=(k == 0), stop=(k == 7))
                evac(p_, tl)

        def conv_pe(wtile, col0, halo, wcols, K, hb, dg, evac2):
            kb.copy(hb[:, 0:K - 1], halo, eng="act")
            proj_fm(None, wtile, col0,
                    lambda p_, tl: kb.act(hb[:, K - 1 + tl.start:K - 1 + tl.stop], p_[:, :], AF.Copy))
            for j in range(K):
                kb.ts(dg[:, j, :], ident_b, wcols[j], None, ALU.mult)
            kb.copy(halo, hb[:, TB:TB + K - 1], eng="act")
            for t in range(TB // 512):
                tl = slice(t * 512, (t + 1) * 512)
                p2 = ps()
                for j in range(K):
                    kb.mm(p2[:, :], dg[:, j, :], hb[:, j + t * 512:j + t * 512 + 512], start=(j == 0), stop=(j == K - 1))
                evac2(p2, tl)

        def ffn(layer):
            A.mark()
            aT = A.bf(NPAIR * TB).rearrange("p (c t) -> p c t", c=NPAIR)
            A.release()
            A.mark()
            A.bf(NPAIR * TB)
            hb = [[A.bf(2 + TB) for _ in range(2)] for _ in range(2)]
            dg = [[A.bf(3 * 128).rearrange("p (j c) -> p j c", j=3) for _ in range(2)] for _ in range(2)]
            sg = [A.f32(TB) for _ in range(2)]
            prenorm(layer, 2)
            for j in range(NPAIR):
                w = wload("up%d_%d" % (layer, j))
                for half in range(2):
                    ch = half * NPAIR + j
                    wc = [PC[:, R_FCW + (layer * 3 + jj) * 44 + ch: R_FCW + (layer * 3 + jj) * 44 + ch + 1]
                          for jj in range(3)]
                    bc = PC[:, R_FCB + layer * 44 + ch: R_FCB + layer * 44 + ch + 1]
                    sgj = sg[j % 2]
                    if half == 0:
                        ev = lambda p2, tl, sgj=sgj, bc=bc: kb.act(sgj[:, tl], p2[:, :], AF.Silu, bias=bc, scale=1.0)
                    else:
                        ev = lambda p2, tl, sgj=sgj, bc=bc, j=j: kb.stt(aT[:, j, tl], p2[:, :], bc, sgj[:, tl],
                                                                      ALU.add, ALU.mult)
                    conv_pe(w, half * 128, halo_f[:, layer, ch, :], wc, 3, hb[j % 2][half], dg[j % 2][half], ev)
            A.release()
            A.mark()
            aT_ = A.bf(NPAIR * TB)
            mix3 = A.f32(8 * TB).rearrange("p (c t) -> p c t", c=8)

            def dn_lhs(oc):
                w0 = wload("dn%d_%d_0" % (layer, oc))
                w1 = wload("dn%d_%d_1" % (layer, oc))
                return [w0[:, k, :] for k in range(11)] + [w1[:, k, :] for k in range(11)]
            out_proj_residual(layer, 3, aT, NPAIR, dn_lhs, mix3)
            A.release()

        def transpose_to_tok(dst_tok, src_fm):
            for tt in range(TB // 128):
                kb.transpose(PT[:, tt * 128:(tt + 1) * 128], src_fm[:, tt * 128:(tt + 1) * 128], ident_b)
            kb.copy(dst_tok.rearrange("p a b -> p (a b)"), PT[:, 0:TB], eng="act")

        def head_norm_out(p_o, src_is_sbuf, gain_col, gate_bf, y_dst):
            kb.act(sqb[:, 0, :], p_o, AF.Square)
            p2 = ps()
            kb.mm(p2[:, :], ones_b, sqb[:, 0, :])
            rstd_from_sum(p2, 1.0 / 128)
            kb.stt(lnv[:, :], p_o, gain_col, rstd[:, :], ALU.mult, ALU.mult)
            kb.tt(y_dst, lnv[:, :], gate_bf, ALU.mult)

        def chunk_rows(c):
            return slice((c % 2) * 64, (c % 2) * 64 + 64)

        def attn_T(at_sb, lhs_fm, rhs_fm, prow, mask, t):
            p_ = ps()
            for c8 in range(8):
                c = t * 8 + c8
                cs = slice(c * 64, (c + 1) * 64)
                kb.mm(p_[chunk_rows(c), (c8 // 2) * 64:(c8 // 2) * 64 + 64], lhs_fm[prow, cs], rhs_fm[prow, cs])
            kb.tt(at_sb, p_[:, 0:256], mask[:, 0:256], ALU.mult)

        def attn_bd(at_sb, lhs_fm, rhs_fm, prow, t):
            p_ = ps()
            for t4 in range(4):
                ts_ = slice((t * 4 + t4) * 128, (t * 4 + t4 + 1) * 128)
                kb.mm(p_[:, t4 * 128:(t4 + 1) * 128], lhs_fm[prow, ts_], rhs_fm[prow, ts_])
            kb.tt(at_sb[:, 0:512], p_[:, :], m2, ALU.mult)

        def run_pipelined(gens):
            import itertools
            cur = None
            for g in itertools.chain(gens, [None]):
                if cur is None:
                    cur = g
                    for tok in cur:
                        if tok == "P2":
                            break
                    continue
                a_done = False
                b_p2 = g is None
                while not (a_done and b_p2):
                    if not a_done:
                        try:
                            next(cur)
                        except StopIteration:
                            a_done = True
                    if not b_p2:
                        try:
                            tok = next(g)
                            if tok == "P2":
                                b_p2 = True
                        except StopIteration:
                            b_p2 = True
                cur = g

        def hgrn2(layer, blk):
            A.mark()
            yT = A.bf(8 * TB).rearrange("p (c t) -> p c t", c=8)
            A.mark()
            T2 = 512
            NC2 = T2 // 64

            class B_:
                pass
            sets = []
            for i in range(2):
                B = B_()
                B.i = i
                B.qs = A.f32(T2)
                B.fb = A.f32(T2)
                B.G = A.f32(T2)
                B.eG = A.f32(T2)
                B.eGi = A.f32(T2)
                B.gs = A.bf(T2)
                B.qd = A.bf(T2)
                B.ki = A.bf(T2)
                B.kd = A.bf(T2)
                B.v_tok = A.bf(T2).rearrange("p (a b) -> p a b", b=128)
                B.kd_tok = A.bf(T2).rearrange("p (a b) -> p a b", b=128)
                B.s32 = A.f32(2 * 128).rearrange("p (a b) -> p a b", b=128)
                B.s_bf = A.bf(NC2 * 128).rearrange("p (a b) -> p a b", b=128)
                B.at_sb = A.bf(512)
                sets.append(B)
            prenorm(layer, 0)
            sc = 128 ** -0.5

            def item(h, sub, B, wqf, wig):
                o = sub * T2
                tl = slice(o, o + T2)
                for (wt, col0, func, dst) in ((wqf, 0, AF.Silu, B.qs), (wqf, 128, AF.Sigmoid, B.fb),
                                              (wig, 128, AF.Silu, B.gs)):
                    p_ = ps()
                    for k in range(8):
                        kb.mm(p_[:, :], wt[:, k, col0:col0 + 128], uT3[:, k, tl], start=(k == 0), stop=(k == 7))
                    kb.act(dst, p_[:, :], func)
                    yield
                p_ = ps()
                for t4 in range(4):
                    for k in range(8):
                        kb.mm(p_[:, t4 * 128:(t4 + 1) * 128], uT3[:, k, o + t4 * 128:o + (t4 + 1) * 128],
                              wig[:, k, 0:128], start=(k == 0), stop=(k == 7))
                kb.copy(B.v_tok.rearrange("p a b -> p (a b)"), p_[:, :], eng="act")
                yield
                kb.ts(B.fb, B.fb, omlc[:, h:h + 1], lbc[:, h:h + 1], ALU.mult, ALU.add)
                kb.act(B.eG, B.fb, AF.Ln)
                kb.scan(B.G, scanm[:, 0:T2], B.eG, 0.0, ALU.mult, ALU.add)
                yield
                kb.act(B.eG, B.G, AF.Exp)
                kb.act(B.eGi, B.G, AF.Exp, scale=-1.0)
                kb.stt(B.qd, B.qs, -sc, B.eG, ALU.mult, ALU.mult)
                yield
                kb.stt(B.ki, B.fb, -1.0, B.eGi, ALU.add, ALU.mult)
                kb.tt(B.kd.rearrange("p (c l) -> p c l", l=64), B.ki.rearrange("p (c l) -> p c l", l=64),
                      B.eG.rearrange("p (c l) -> p c l", l=64)[:, :, 63:64].to_broadcast([128, NC2, 64]), ALU.mult)
                yield
                for t4 in range(4):
                    kb.transpose(PT[:, B.i * 512 + t4 * 128:B.i * 512 + (t4 + 1) * 128],
                                 B.kd[:, t4 * 128:(t4 + 1) * 128], ident_b)
                kb.copy(B.kd_tok.rearrange("p a b -> p (a b)"), PT[:, B.i * 512:(B.i + 1) * 512], eng="act")
                yield "P2"
                kb.copy(B.s32[:, 0, :], st_h[:, h, :], eng="dve")
                for c in range(NC2):
                    tt = c // 2
                    cr = chunk_rows(c)
                    kb.copy(B.s_bf[:, c, :], B.s32[:, c % 2, :], eng="act")
                    pkv = ps()
                    kb.mm(pkv[:, 0:128], B.kd_tok[cr, tt, :], B.v_tok[cr, tt, :])
                    kb.stt(B.s32[:, (c + 1) % 2, :], B.s32[:, c % 2, :], B.eG[:, c * 64 + 63:c * 64 + 64],
                           pkv[:, 0:128], ALU.mult, ALU.add)
                    yield
                kb.copy(st_h[:, h, :], B.s32[:, NC2 % 2, :], eng="dve")
                p_ = ps()
                for t4 in range(4):
                    ts_ = slice(t4 * 128, (t4 + 1) * 128)
                    kb.mm(p_[:, ts_], B.ki[:, ts_], B.qd[:, ts_])
                kb.tt(B.at_sb[:, 0:512], p_[:, :], m2, ALU.mult)
                yield
                po = ps()
                for t4 in range(4):
                    oc_ = slice(t4 * 128, (t4 + 1) * 128)
                    kb.mm(po[:, oc_], B.v_tok[:, t4, :], B.at_sb[:, oc_], start=True, stop=False)
                    for cc in range(2):
                        c = t4 * 2 + cc
                        kb.mm(po[:, t4 * 128 + cc * 64:t4 * 128 + cc * 64 + 64], B.s_bf[:, c, :],
                              B.qd[:, c * 64:(c + 1) * 64], start=False, stop=(cc == 1))
                yield
                head_norm_out(po[:, :], False, PC[:, R_CN + h:R_CN + h + 1], B.gs, yT[:, h, tl])
                yield

            def all_items():
                n = 0
                for h in range(8):
                    wqf = wload("cqf%d" % h)
                    wig = wload("cig%d" % h)
                    for sub in range(TB // T2):
                        yield item(h, sub, sets[n % 2], wqf, wig)
                        n += 1
            run_pipelined(all_items())
            A.release()
            mix3 = A.f32(8 * TB).rearrange("p (c t) -> p c t", c=8)

            def lhs(oc):
                if oc % 2 == 0:
                    lhs.w = wload("cwo%d" % (oc // 2))
                return [lhs.w[:, k, (oc % 2) * 128:(oc % 2) * 128 + 128] for k in range(8)]
            out_proj_residual(layer, 1, yT, 8, lhs, mix3)
            A.release()

        def conv_silu_chunk(wtile, col0, halo, wcols, dst, hb, dg):
            conv_pe(wtile, col0, halo, wcols, 4, hb, dg, lambda p2, tl: kb.act(dst[:, tl], p2[:, :], AF.Silu))

        def ab_mixer(layer, blk):
            A.mark()
            yT = A.bf(8 * TB).rearrange("p (c t) -> p c t", c=8)
            A.mark()
            prenorm(layer, 0)
            Gg = A.f32(TB)
            SG = A.f32(TB)
            Gtok = A.f32(8 * 16).rearrange("p (a b) -> p a b", b=16)
            A.mark()
            T15 = A.f32(TB)
            bm = A.f32(TB)
            A.mark()
            pre = A.f32(TB)
            tmp = A.f32(TB)
            r16 = slice(0, 16)
            wg = wload("gates")
            for t in range(TB // 512):
                tl = slice(t * 512, (t + 1) * 512)
                p_ = ps()
                for k in range(8):
                    kb.mm(p_[r16, :], wg[:, k, :], uT3[:, k, tl], start=(k == 0), stop=(k == 7))
                kb.act(pre[r16, tl], p_[r16, :], AF.Identity, bias=gsc[r16, 0:1], scale=1.0)
            kb.act(T15[r16, :], pre[r16, :], AF.Tanh, scale=1.0 / 15.0)
            kb.ts(T15[r16, :], T15[r16, :], 15.0, None, ALU.mult)
            kb.act(tmp[r16, :], T15[r16, :], AF.Exp, scale=-1.0)
            kb.act(tmp[r16, :], tmp[r16, :], AF.Ln, bias=one_c[r16, :], scale=1.0)
            kb.scan(bm[r16, :], scanm[r16, :], tmp[r16, :], 0.0, ALU.mult, ALU.add)
            kb.act(SG[r16, :], pre[r16, :], AF.Sigmoid)
            kb.act(tmp[r16, :], pre[r16, :], AF.Exp)
            kb.act(tmp[r16, :], tmp[r16, :], AF.Ln, bias=one_c[r16, :], scale=1.0)
            kb.ts(tmp[r16, :], tmp[r16, :], nac[r16, :], None, ALU.mult)
            kb.scan(Gg[r16, :], scanm[r16, :], tmp[r16, :], 0.0, ALU.mult, ALU.add)
            for tt in range(8):
                p_ = ps()
                kb.transpose(p_[:, 0:16], Gg[r16, tt * 128:(tt + 1) * 128], ident_f[r16, 0:16])
                kb.copy(Gtok[:, tt, :], p_[:, 0:16], eng="act")
            A.release()

            hb = A.bf(4 + TB)
            dg = A.bf(4 * 128).rearrange("p (j c) -> p j c", j=4)
            acc = A.f32(TB)
            ks = A.f32(TB)
            Abc = A.f32(TB)
            Cbc = A.f32(TB)
            qd = A.bf(TB)
            kt = A.bf(TB)
            kd = A.bf(TB)
            kd_tok = A.bf(TB).rearrange("p (a b) -> p a b", b=128)
            v_tok = A.bf(8 * 256).rearrange("p (a b) -> p a b", b=256)
            og = A.bf(2 * TB).rearrange("p (h t) -> p h t", h=2)
            c32 = A.f32(2 * 256).rearrange("p (a b) -> p a b", b=256)
            c_bf = A.bf(NCH * 256).rearrange("p (a b) -> p a b", b=256)
            at_sb = A.bf(512)
            hm = A.f32(512)
            ad = A.f32(512)
            sc = 64 ** -0.5
            for pr in range(2):
                wv = wload("mv%d" % pr)
                for tt in range(8):
                    if tt % 2 == 0:
                        p_ = ps()
                    for k in range(8):
                        kb.mm(p_[:, (tt % 2) * 256:(tt % 2) * 256 + 256], uT3[:, k, tt * 128:(tt + 1) * 128],
                              wv[:, k, :], start=(k == 0), stop=(k == 7))
                    if tt % 2 == 1:
                        kb.copy(v_tok[:, tt - 1:tt + 1, :], p_[:, :].rearrange("p (a b) -> p a b", b=256), eng="act")
                wo_ = wload("mo%d" % pr)
                for hh in range(2):
                    proj_fm(None, wo_, hh * 128, lambda p_, tl, hh=hh: kb.act(og[:, hh, tl], p_[:, :], AF.Sigmoid))
                cw_q = [PC[:, R_CM + j * 4 + pr: R_CM + j * 4 + pr + 1] for j in range(4)]
                cw_k = [PC[:, R_CM + j * 4 + 2 + pr: R_CM + j * 4 + 2 + pr + 1] for j in range(4)]
                for t in range(TB // 512):
                    tl = slice(t * 512, (t + 1) * 512)
                    p_ = ps()
                    kb.mm(p_[:, :], sel(2 + pr), bm[r16, tl])
                    kb.act(Abc[:, tl], p_[:, :], AF.Exp, scale=-1.0)
                    p2 = ps()
                    kb.mm(p2[:, :], sel(pr), T15[r16, tl], start=True, stop=False)
                    kb.mm(p2[:, :], sel(2 + pr), bm[r16, tl], start=False, stop=True)
                    kb.act(Cbc[:, tl], p2[:, :], AF.Exp)
                wq = wload("mqk0")
                wk = wload("mqk1")
                conv_silu_chunk(wq, pr * 128, halo_m[:, pr, :], cw_q, acc, hb, dg)
                kb.stt(qd, acc, sc, Abc, ALU.mult, ALU.mult)
                conv_silu_chunk(wk, pr * 128, halo_m[:, 2 + pr, :], cw_k, ks, hb, dg)
                kb.tt(kt, ks, Cbc, ALU.mult)
                kb.tt(kd.rearrange("p (c l) -> p c l", l=64), kt.rearrange("p (c l) -> p c l", l=64),
                      Abc.rearrange("p (c l) -> p c l", l=64)[:, :, 63:64].to_broadcast([128, NCH, 64]), ALU.mult)
                transpose_to_tok(kd_tok, kd)
                kb.copy(c32[:, 0, :], st_m[:, pr, :], eng="dve")
                for c in range(NCH):
                    tt = c // 2
                    cr = chunk_rows(c)
                    kb.copy(c_bf[:, c, :], c32[:, c % 2, :], eng="act")
                    pkv = ps()
                    o0 = 0
                    for hh in range(2):
                        hr_ = slice(hh * 64, hh * 64 + 64)
                        kb.mm(pkv[hr_, o0:o0 + 128], kd_tok[cr, tt, hh * 64:hh * 64 + 64],
                              v_tok[cr, tt, hh * 128:(hh + 1) * 128])
                        kb.mm(pkv[hr_, o0 + 128:o0 + 256], kd_tok[cr, tt, hh * 64:hh * 64 + 64], ones_b[cr, :])
                    kb.stt(c32[:, (c + 1) % 2, :], c32[:, c % 2, :], Abc[:, c * 64 + 63:c * 64 + 64],
                           pkv[:, o0:o0 + 256], ALU.mult, ALU.add)
                kb.copy(st_m[:, pr, :], c32[:, NCH % 2, :], eng="dve")
                for hh in range(2):
                    h = pr * 2 + hh
                    hr_ = slice(hh * 64, hh * 64 + 64)
                    for t in range(TB // 512):
                        tl = slice(t * 512, (t + 1) * 512)
                        attn_bd(at_sb, kt, qd, hr_, t)
                        po = ps()
                        pd = ps()
                        for t4 in range(4):
                            tt = t * 4 + t4
                            oc_ = slice(t4 * 128, (t4 + 1) * 128)
                            kb.mm(po[:, oc_], v_tok[:, tt, hh * 128:(hh + 1) * 128], at_sb[:, oc_], start=True, stop=False)
                            for cc in range(2):
                                c = tt * 2 + cc
                                kb.mm(po[:, t4 * 128 + cc * 64:t4 * 128 + cc * 64 + 64], c_bf[hr_, c, 0:128],
                                      qd[hr_, c * 64:(c + 1) * 64], start=False, stop=(cc == 1))
                            kb.mm(pd[:, oc_], ones_b, at_sb[:, oc_], start=True, stop=False)
                            for cc in range(2):
                                c = tt * 2 + cc
                                kb.mm(pd[:, t4 * 128 + cc * 64:t4 * 128 + cc * 64 + 64], c_bf[hr_, c, 128:256],
                                      qd[hr_, c * 64:(c + 1) * 64], start=False, stop=(cc == 1))
                        kb.act(ad, pd[:, :], AF.Abs)
                        kb.ts(ad, ad, 1.0, None, ALU.max)
                        kb.recip(ad, ad)
                        kb.tt(hm, po[:, :], ad, ALU.mult)
                        head_norm_out(hm, True, PC[:, R_MN + h:R_MN + h + 1], og[:, hh, tl], yT[:, h, tl])
            A.release()
            gdn(layer, blk, yT, Gg, Gtok, SG)
            A.release()
            mix3 = A.f32(8 * TB).rearrange("p (c t) -> p c t", c=8)

            def lhs(oc):
                if oc % 2 == 0:
                    lhs.w = wload("abwo%d" % (oc // 2))
                return [lhs.w[:, k, (oc % 2) * 128:(oc % 2) * 128 + 128] for k in range(8)]
            if dbg_d is not None:
                dv = dbg_d[3].rearrange("(c p) t -> p c t", p=128)
                for c in range(8):
                    kb.copy(mix3[:, c, :], yT[:, c, :], eng="dve")
                    kb.dma(dv[:, c, blk * TB:(blk + 1) * TB], mix3[:, c, :], eng="sp", is_out=True)
            out_proj_residual(layer, 1, yT, 8, lhs, mix3)
            A.release()

        def gdn(layer, blk, yT, Gg, Gtok, SG):
            A.mark()
            r16 = slice(0, 16)
            hb = A.bf(4 + TB)
            dg = A.bf(4 * 128).rearrange("p (j c) -> p j c", j=4)
            acc = A.f32(TB)
            kn = A.f32(TB)
            eG = A.f32(TB)
            eGd = A.f32(TB)
            beta = A.f32(TB)
            gg = A.bf(TB)
            qnb = A.bf(TB)
            qd = A.bf(TB)
            knb = A.bf(TB)
            kbb = A.bf(TB)
            kbG = A.bf(TB)
            kd = A.bf(TB)
            qp = kbb
            kd_tok = A.bf(TB).rearrange("p (a b) -> p a b", b=128)
            R_flat = A.bf(8 * 256)
            R_tok = R_flat.rearrange("p (a b) -> p a b", b=256)
            KcT = R_flat.rearrange("p (a b) -> p a b", b=128)
            X_tok = A.bf(8 * 256).rearrange("p (a b) -> p a b", b=256)
            scrA = A.bf(2 * TB)
            tmp_tok = scrA[:, 0:TB].rearrange("p (a b) -> p a b", b=128)
            vb = scrA[:, TB:2 * TB]
            s_bf = scrA.rearrange("p (a b) -> p a b", b=128)
            dtmp = acc[:, 512:1024]
            YT = dtmp
            DT = A.f32(512)
            DTs = A.f32(512)
            YTb = A.bf(512)
            Pm = A.bf(512)
            PTm = A.bf(512)
            P2 = A.bf(512)
            at_sb = A.bf(512)
            at2 = A.bf(8 * 128).rearrange("p (a b) -> p a b", b=128)
            D2 = acc[:, 0:512]
            s32 = A.f32(2 * 128).rearrange("p (a b) -> p a b", b=128)
            rq = DT
            kb.memset(at2, 0.0, eng="dve")
            sc = 128 ** -0.5
            wn = {"q": ("gq", 0), "k": ("gk", 4), "v": ("gv", 8)}

            for h in range(4):
                half, hh = h // 2, h % 2
                for t in range(TB // 512):
                    tl = slice(t * 512, (t + 1) * 512)
                    p_ = ps()
                    kb.mm(p_[:, :], sel(4 + h), Gg[r16, tl])
                    kb.copy(kn[:, tl], p_[:, :], eng="act")
                    kb.act(eG[:, tl], p_[:, :], AF.Exp)
                    p2 = ps()
                    kb.mm(p2[:, :], sel(8 + h), SG[r16, tl])
                    kb.copy(beta[:, tl], p2[:, :], eng="act")
                kb.tt(eGd.rearrange("p (c l) -> p c l", l=64),
                      kn.rearrange("p (c l) -> p c l", l=64)[:, :, 63:64].to_broadcast([128, NCH, 64]),
                      kn.rearrange("p (c l) -> p c l", l=64), ALU.subtract)
                kb.act(eGd, eGd, AF.Exp)
                wgg = wload("gg%d" % half)
                proj_fm(None, wgg, hh * 128, lambda p_, tl: kb.act(gg[:, tl], p_[:, :], AF.Silu))

                def conv_of(kind):
                    nm, cbase = wn[kind]
                    w = wload("%s%d" % (nm, half))
                    ch = cbase + h
                    wc = [PC[:, R_CG + j * 12 + ch: R_CG + j * 12 + ch + 1] for j in range(4)]
                    conv_silu_chunk(w, hh * 128, halo_g[:, ch, :], wc, acc, hb, dg)

                def l2rstd(t):
                    tl = slice(t * 512, (t + 1) * 512)
                    kb.act(sqb[:, 0, :], acc[:, tl], AF.Square)
                    p_ = ps()
                    kb.mm(p_[:, :], ones_b, sqb[:, 0, :])
                    kb.act(lnv[:, :], p_[:, :], AF.Ln, bias=eps_c, scale=1.0)
                    kb.act(rq[:, :], lnv[:, :], AF.Exp, scale=-0.5)
                conv_of("q")
                for t in range(TB // 512):
                    tl = slice(t * 512, (t + 1) * 512)
                    l2rstd(t)
                    kb.stt(acc[:, tl], acc[:, tl], sc, rq, ALU.mult, ALU.mult)
                kb.copy(qnb, acc, eng="pool")
                kb.tt(qd, acc, eG, ALU.mult)
                conv_of("k")
                for t in range(TB // 512):
                    tl = slice(t * 512, (t + 1) * 512)
                    l2rstd(t)
                    kb.tt(kn[:, tl], acc[:, tl], rq, ALU.mult)
                kb.copy(knb, kn, eng="pool")
                kb.tt(kd, kn, eGd, ALU.mult)
                kb.tt(kn, kn, beta, ALU.mult)
                kb.copy(kbb, kn, eng="pool")
                kb.tt(kbG, kn, eG, ALU.mult)
                conv_of("v")
                kb.tt(vb, acc, beta, ALU.mult)
                pg = ps()
                for tt in range(8):
                    for cc in range(2):
                        c = tt * 2 + cc
                        kb.mm(pg[cc * 64:(cc + 1) * 64, tt * 64:(tt + 1) * 64], sel(4 + h)[:, cc * 64:(cc + 1) * 64],
                              Gg[r16, c * 64:(c + 1) * 64])
                for tt in range(8):
                    kb.stt(dtmp[:, tt * 64:(tt + 1) * 64], pg[:, tt * 64:(tt + 1) * 64],
                           Gtok[:, tt, 8 + h:9 + h], negm[:, 0:64], ALU.subtract, ALU.add)
                kb.act(DT, dtmp, AF.Exp)
                kb.tt(DTs, DT, m_su, ALU.mult)
                for tt in range(8):
                    kb.stt(dtmp[:, tt * 64:(tt + 1) * 64], pg[:, tt * 64:(tt + 1) * 64],
                           Gtok[:, tt, 8 + h:9 + h], posm[:, 0:64], ALU.subtract, ALU.add)
                kb.act(D2, dtmp, AF.Exp, scale=-1.0)
                transpose_to_tok(kd_tok, kd)
                transpose_to_tok(tmp_tok, vb)
                kb.copy(R_tok[:, :, 0:128], tmp_tok, eng="pool")
                transpose_to_tok(tmp_tok, kbG)
                kb.copy(R_tok[:, :, 128:256], tmp_tok, eng="pool")
                for t in range(TB // 512):
                    pa = ps()
                    pb = ps()
                    pc_ = ps()
                    for c8 in range(8):
                        c = t * 8 + c8
                        cs = slice(c * 64, (c + 1) * 64)
                        o_ = slice((c8 // 2) * 64, (c8 // 2) * 64 + 64)
                        kb.mm(pa[chunk_rows(c), o_], knb[:, cs], kbb[:, cs])
                        kb.mm(pb[chunk_rows(c), o_], kbb[:, cs], knb[:, cs])
                        kb.mm(pc_[chunk_rows(c), o_], knb[:, cs], qnb[:, cs])
                    tw = slice(t * 256, (t + 1) * 256)
                    kb.tt(PTm[:, tw], pa[:, 0:256], DTs[:, tw], ALU.mult)
                    kb.tt(Pm[:, tw], pb[:, 0:256], D2[:, tw], ALU.mult)
                    kb.tt(at_sb[:, tw], pc_[:, 0:256], DT[:, tw], ALU.mult)
                W512 = slice(0, 512)
                lo, hi = slice(0, 64), slice(64, 128)
                kb.copy(at2[lo, :, 0:64], at_sb[lo, :].rearrange("p (a b) -> p a b", b=64), eng="pool")
                kb.copy(at2[hi, :, 64:128], at_sb[hi, :].rearrange("p (a b) -> p a b", b=64), eng="pool")
                kb.tt(YT[:, W512], iblk, PTm[:, W512], ALU.subtract)
                kb.copy(YTb[:, W512], YT[:, W512], eng="act")

                def blockmm(dst_pair, lhs, rhs):
                    for c in range(NCH):
                        cr = chunk_rows(c)
                        o_ = slice((c // 2) * 64, (c // 2) * 64 + 64)
                        kb.mm(dst_pair[c % 2][cr, o_], lhs[cr, o_], rhs[cr, o_])
                for step in range(5):
                    pa2 = [ps(), ps()]
                    blockmm(pa2, PTm, Pm)
                    pb2 = [ps(), ps()]
                    blockmm(pb2, Pm, PTm)
                    kb.copy(P2[lo, W512], pa2[0][lo, :], eng="act")
                    kb.copy(P2[hi, W512], pa2[1][hi, :], eng="act")
                    kb.copy(PTm[lo, W512], pb2[0][lo, :], eng="dve")
                    kb.copy(PTm[hi, W512], pb2[1][hi, :], eng="dve")
                    kb.copy(Pm[:, W512], P2[:, W512], eng="pool")
                    pc2 = [ps(), ps()]
                    blockmm(pc2, P2, YTb)
                    kb.tt(YT[lo, W512], YT[lo, W512], pc2[0][lo, :], ALU.add)
                    kb.tt(YT[hi, W512], YT[hi, W512], pc2[1][hi, :], ALU.add)
                    if step < 4:
                        kb.copy(YTb[:, W512], YT[:, W512], eng="act")
                kb.tt(YTb[:, W512], YT[:, W512], iblk, ALU.subtract)
                for tt in range(8):
                    if tt % 2 == 0:
                        px2 = [ps(), ps()]
                    for cc in range(2):
                        c = tt * 2 + cc
                        cr = chunk_rows(c)
                        o_ = slice((c // 2) * 64, (c // 2) * 64 + 64)
                        kb.mm(px2[cc][cr, (tt % 2) * 256:(tt % 2) * 256 + 256], YTb[cr, o_], R_tok[cr, tt, :])
                    if tt % 2 == 1:
                        for cc, hs in ((0, lo), (1, hi)):
                            kb.tt(X_tok[hs, tt - 1:tt + 1, :], px2[cc][hs, :].rearrange("p (a b) -> p a b", b=256),
                                  R_tok[hs, tt - 1:tt + 1, :], ALU.add)
                for t in range(TB // 512):
                    tl = slice(t * 512, (t + 1) * 512)
                    pq = ps()
                    for t4 in range(4):
                        tt = t * 4 + t4
                        kb.mm(pq[:, t4 * 128:(t4 + 1) * 128], X_tok[:, tt, 128:256], at2[:, tt, :])
                    kb.tt(qp[:, tl], qd[:, tl], pq[:, :], ALU.subtract)
                KcT4 = R_flat.rearrange("p (a two b) -> p a two b", two=2, b=128)
                for g in range(2):
                    pk2 = [ps(), ps()]
                    for c8 in range(8):
                        c = g * 8 + c8
                        cr = chunk_rows(c)
                        kb.mm(pk2[c % 2][:, (c8 // 2) * 128:(c8 // 2 + 1) * 128], X_tok[cr, c // 2, 128:256],
                              kd_tok[cr, c // 2, :])
                    for cc in range(2):
                        kb.act(KcT4[:, g * 4:(g + 1) * 4, cc, :], pk2[cc][:, :].rearrange("p (a b) -> p a b", b=128),
                               AF.Copy, scale=-1.0)
                kb.copy(s32[:, 0, :], st_g[:, h, :], eng="dve")
                for c in range(NCH):
                    cr = chunk_rows(c)
                    kb.copy(s_bf[:, c, :], s32[:, c % 2, :], eng="act")
                    pss = ps()
                    o_ = slice(0, 128)
                    kb.mm(pss[:, o_], kd_tok[cr, c // 2, :], X_tok[cr, c // 2, 0:128], start=True, stop=False)
                    kb.mm(pss[:, o_], KcT[:, c, :], s_bf[:, c, :], start=False, stop=True)
                    kb.stt(s32[:, (c + 1) % 2, :], s32[:, c % 2, :], eG[:, c * 64 + 63:c * 64 + 64],
                           pss[:, o_], ALU.mult, ALU.add)
                kb.copy(st_g[:, h, :], s32[:, NCH % 2, :], eng="dve")
                for t in range(TB // 512):
                    tl = slice(t * 512, (t + 1) * 512)
                    po = ps()
                    for t4 in range(4):
                        tt = t * 4 + t4
                        oc_ = slice(t4 * 128, (t4 + 1) * 128)
                        kb.mm(po[:, oc_], X_tok[:, tt, 0:128], at2[:, tt, :], start=True, stop=False)
                        for cc in range(2):
                            c = tt * 2 + cc
                            kb.mm(po[:, t4 * 128 + cc * 64:t4 * 128 + cc * 64 + 64], s_bf[:, c, :],
                                  qp[:, c * 64:(c + 1) * 64], start=False, stop=(cc == 1))
                    head_norm_out(po[:, :], False, PC[:, R_GN + h:R_GN + h + 1], gg[:, tl], yT[:, 4 + h, tl])
            A.release()

        xv = xT_d.rearrange("(c p) t -> p c t", p=128)
        ov = oT_d.rearrange("(c p) t -> p c t", p=128)

        def checkpoint(i, blk):
            if dbg_d is not None:
                dv = dbg_d[i].rearrange("(c p) t -> p c t", p=128)
                for c in range(8):
                    kb.dma(dv[:, c, blk * TB:(blk + 1) * TB], xT[:, c, :], eng="sp", is_out=True)

        for blk in range(NBLK):
            for c in range(8):
                kb.dma(xT[:, c, :], xv[:, c, blk * TB:(blk + 1) * TB])
            if "mix0" not in (stop_after or ()):
                ab_mixer(0, blk)
            checkpoint(0, blk)
            if "ffn0" not in (stop_after or ()):
                ffn(0)
            checkpoint(1, blk)
            if n_layers > 1:
                if "mix1" not in (stop_after or ()):
                    hgrn2(1, blk)
                checkpoint(2, blk)
                if "ffn1" not in (stop_after or ()):
                    ffn(1)
            for c in range(8):
                kb.dma(ov[:, c, blk * TB:(blk + 1) * TB], xT[:, c, :], eng="sp", is_out=True)
        kb.P.finalize()
        kb.P.emit(nc, block, sems)
        print("program ops:", len(kb.P.ops), "arena top:", A.top, flush=True)
    return nc


_CACHE = {}


def _host_inputs(inputs):
    WP = pack_weights(inputs)
    pv, gsc = pack_params(inputs)
    cb, cf = make_consts()
    return WP, pv, gsc, cb, cf


def kernel(**inputs):
    inputs = {k: np.asarray(v) for k, v in inputs.items()}
    x = inputs["x"].astype(np.float32, copy=False)
    WP, pv, gsc, cb, cf = _host_inputs(inputs)
    if "nc" not in _CACHE:
        _CACHE["nc"] = build_program()
    nc = _CACHE["nc"]
    in_maps = []
    for b in range(8):
        in_maps.append({"xT": np.ascontiguousarray(x[b].T), "wp": WP, "pv": pv, "gsc": gsc, "cb": cb, "cf": cf})
    res = run_bass_kernel_spmd(nc, in_maps, core_ids=list(range(8)))
    out = np.stack([np.ascontiguousarray(res.results[b]["oT"].T) for b in range(8)], axis=0)
    return out.astype(np.float32)
```
